# Optimizing a Trainium2 kernel written in Bass

```python
import jax, jax.numpy as jnp
from jax import lax
import numpy as np

D_MODEL = 2048
BATCH = 16
SEQ = 256
DEPTH = 2
DEC_BATCH = 2
DEC_SEQ = 4096
PAST_LEN = 512

GRID_W = 64
HEAD_DIM = 128
N_HEADS = D_MODEL // HEAD_DIM
N_KV_HEADS = N_HEADS // 4
GROUP = N_HEADS // N_KV_HEADS
WINDOW = 128
BLOCK = 128
ROPE_BASE = 10000.0
D_CONV_A = D_MODEL // 2
CONV_A_WIDTH = 31
D_CONV_B = D_MODEL // 2
CONV_B_WIDTH = 3
N_BRANCH = 3
D_FF = 5632
EPS = 1e-6
NEG_INF = -1e30
Q_DIM = N_HEADS * HEAD_DIM
KV_DIM = N_KV_HEADS * HEAD_DIM
IN_COLS = Q_DIM + 2 * KV_DIM + 2 * D_CONV_A + 3 * D_CONV_B + N_BRANCH * D_MODEL
SPLIT_IDX = (Q_DIM, Q_DIM + KV_DIM, Q_DIM + 2 * KV_DIM,
             Q_DIM + 2 * KV_DIM + 2 * D_CONV_A,
             Q_DIM + 2 * KV_DIM + 2 * D_CONV_A + 3 * D_CONV_B)
N_MOD = 9

kernel_name = "hybrid_diffusion_prefix_step"


def _rms_norm(x, g):
    xf = x.astype(jnp.float32)
    xf = xf * lax.rsqrt(jnp.mean(xf * xf, axis=-1, keepdims=True) + EPS)
    return xf.astype(x.dtype) * g


def _ada(cvec, w, b):
    m = jax.nn.silu(cvec) @ w + b
    return [t[:, None, :] for t in jnp.split(m, N_MOD, axis=-1)]


def _ffn_half(x, shift, scale, gate, g, w_in, w_out):
    h = _rms_norm(x, g) * (1 + scale) + shift
    u, v = jnp.split(h @ w_in, 2, axis=-1)
    return x + 0.5 * gate * ((jax.nn.silu(u) * v) @ w_out)


def _dwconv(x, w, width):
    pad = (width - 1) // 2
    return lax.conv_general_dilated(x, w, window_strides=(1,), padding=[(pad, pad)],
                                    dimension_numbers=('NWC', 'WIO', 'NWC'),
                                    feature_group_count=x.shape[-1])


def _rope_1d(x, ang):
    x1, x2 = jnp.split(x, 2, axis=-1)
    cos = jnp.cos(ang)[None, :, None, :]
    sin = jnp.sin(ang)[None, :, None, :]
    return jnp.concatenate([x1 * cos - x2 * sin, x2 * cos + x1 * sin], axis=-1)


def _rope_2d(x, rows):
    n_freq = HEAD_DIM // 4
    inv = ROPE_BASE ** (-jnp.arange(n_freq, dtype=jnp.float32) / n_freq)
    r = jnp.repeat(jnp.arange(rows, dtype=jnp.float32), GRID_W)
    col = jnp.tile(jnp.arange(GRID_W, dtype=jnp.float32), rows)
    xf = x.astype(jnp.float32)
    xr, xc = jnp.split(xf, 2, axis=-1)
    out = jnp.concatenate([_rope_1d(xr, r[:, None] * inv), _rope_1d(xc, col[:, None] * inv)], axis=-1)
    return out.astype(x.dtype)


def _sink_combine(parts, sink):
    b, hk, g, q = parts[0][0].shape[:4]
    sink_col = jnp.broadcast_to(sink.reshape(1, hk, g, 1, 1).astype(jnp.float32), (b, hk, g, q, 1))
    p = jax.nn.softmax(jnp.concatenate([s for s, _ in parts] + [sink_col], axis=-1), axis=-1)
    out = None
    off = 0
    for s, v in parts:
        n = s.shape[-1]
        o = jnp.einsum('bhgqk,bkhd->bqhgd', p[..., off:off + n].astype(v.dtype), v)
        out = o if out is None else out + o
        off += n
    return out


def _q_blocks(q):
    b, l = q.shape[:2]
    nblk = l // BLOCK
    qb = q.reshape(b, nblk, BLOCK, N_KV_HEADS, GROUP, HEAD_DIM) * (HEAD_DIM ** -0.5)
    return jnp.transpose(qb, (1, 0, 2, 3, 4, 5)), nblk


def _unblock(out, b, l):
    return jnp.transpose(out, (1, 0, 2, 3, 4, 5)).reshape(b, l, Q_DIM)


def _context_attention(q, k, v, sink):
    b, l = q.shape[:2]
    qb, _ = _q_blocks(q)

    def one(q_blk):
        s = jnp.einsum('bqhgd,bkhd->bhgqk', q_blk, k).astype(jnp.float32)
        return _sink_combine([(s, v)], sink)

    return _unblock(lax.map(one, qb), b, l)


def _latent_attention(q, k, v, k_ctx, v_ctx, sink):
    b, l = q.shape[:2]
    qb, nblk = _q_blocks(q)
    k_pad = jnp.pad(k, ((0, 0), (BLOCK, BLOCK), (0, 0), (0, 0)))
    v_pad = jnp.pad(v, ((0, 0), (BLOCK, BLOCK), (0, 0), (0, 0)))

    def one(args):
        q_blk, i = args
        start = i * BLOCK
        kw = lax.dynamic_slice_in_dim(k_pad, start, 3 * BLOCK, axis=1)
        vw = lax.dynamic_slice_in_dim(v_pad, start, 3 * BLOCK, axis=1)
        qp = start + jnp.arange(BLOCK)
        kp = start - BLOCK + jnp.arange(3 * BLOCK)
        valid = (kp >= 0)[None, :] & (kp < l)[None, :] & (jnp.abs(qp[:, None] - kp[None, :]) <= WINDOW)
        s_w = jnp.einsum('bqhgd,bkhd->bhgqk', q_blk, kw).astype(jnp.float32)
        s_w = jnp.where(valid, s_w, NEG_INF)
        s_c = jnp.einsum('bqhgd,bkhd->bhgqk', q_blk, k_ctx).astype(jnp.float32)
        return _sink_combine([(s_w, vw), (s_c, v_ctx)], sink)

    return _unblock(lax.map(one, (qb, jnp.arange(nblk))), b, l)


def _mix_project(x, shift, scale, g_mix, w_in):
    b, l = x.shape[:2]
    h = _rms_norm(x, g_mix) * (1 + scale) + shift
    q, k, v, a_in, bcx, gates = jnp.split(h @ w_in, SPLIT_IDX, axis=-1)
    q = q.reshape(b, l, N_HEADS, HEAD_DIM)
    k = k.reshape(b, l, N_KV_HEADS, HEAD_DIM)
    v = v.reshape(b, l, N_KV_HEADS, HEAD_DIM)
    return q, k, v, a_in, bcx, gates


def _mix_merge(x, gate, attn, a_in, bcx, gates, w_attn_o, conv_a_w, conv_a_b, g_conv_a, w_a_out,
               conv_b_w, w_b_out, w_out):
    a, a_g = jnp.split(a_in, 2, axis=-1)
    a = _dwconv(a * jax.nn.sigmoid(a_g), conv_a_w, CONV_A_WIDTH) + conv_a_b
    br_a = jax.nn.silu(_rms_norm(a, g_conv_a)) @ w_a_out
    bg, cg, xv = jnp.split(bcx, 3, axis=-1)
    br_b = (bg * _dwconv(cg * xv, conv_b_w, CONV_B_WIDTH)) @ w_b_out
    br_c = attn @ w_attn_o
    ga, gb, gc = jnp.split(jax.nn.sigmoid(gates), 3, axis=-1)
    return x + gate * ((ga * br_a + gb * br_b + gc * br_c) @ w_out)


def setup_inputs(seed: int = 0) -> dict:
    key = jax.random.key(seed)
    ks = jax.random.split(key, 26)

    def nrm(k, shape, s):
        return jax.random.normal(k, shape, jnp.float32) * s

    kv_shape = (DEC_BATCH, DEPTH, PAST_LEN, N_KV_HEADS, HEAD_DIM)
    return {
        "x_prompt": nrm(ks[0], (BATCH, SEQ, D_MODEL), 1.0),
        "x_sample": nrm(ks[1], (DEC_BATCH, DEC_SEQ, D_MODEL), 1.0),
        "cache_k": nrm(ks[2], kv_shape, 1.0),
        "cache_v": nrm(ks[3], kv_shape, 1.0),
        "c": nrm(ks[4], (DEC_BATCH, D_MODEL), 1.0),
        "c_ctx": nrm(ks[5], (D_MODEL,), 1.0),
        "w_ada": nrm(ks[6], (DEPTH, D_MODEL, N_MOD * D_MODEL), D_MODEL ** -0.5),
        "b_ada": nrm(ks[7], (DEPTH, N_MOD * D_MODEL), 0.01),
        "g_ff1": 1.0 + nrm(ks[8], (DEPTH, D_MODEL), 0.1),
        "w_ff1_in": nrm(ks[9], (DEPTH, D_MODEL, 2 * D_FF), D_MODEL ** -0.5),
        "w_ff1_out": nrm(ks[10], (DEPTH, D_FF, D_MODEL), D_FF ** -0.5),
        "g_mix": 1.0 + nrm(ks[11], (DEPTH, D_MODEL), 0.1),
        "w_in": nrm(ks[12], (DEPTH, D_MODEL, IN_COLS), D_MODEL ** -0.5),
        "attn_sink": nrm(ks[13], (DEPTH, N_HEADS), 1.0),
        "w_attn_o": nrm(ks[14], (DEPTH, Q_DIM, D_MODEL), Q_DIM ** -0.5),
        "conv_a_w": nrm(ks[15], (DEPTH, CONV_A_WIDTH, 1, D_CONV_A), CONV_A_WIDTH ** -0.5),
        "conv_a_b": nrm(ks[16], (DEPTH, D_CONV_A), 0.01),
        "g_conv_a": 1.0 + nrm(ks[17], (DEPTH, D_CONV_A), 0.1),
        "w_a_out": nrm(ks[18], (DEPTH, D_CONV_A, D_MODEL), D_CONV_A ** -0.5),
        "conv_b_w": nrm(ks[19], (DEPTH, CONV_B_WIDTH, 1, D_CONV_B), CONV_B_WIDTH ** -0.5),
        "w_b_out": nrm(ks[20], (DEPTH, D_CONV_B, D_MODEL), D_CONV_B ** -0.5),
        "w_out": nrm(ks[21], (DEPTH, D_MODEL, D_MODEL), D_MODEL ** -0.5),
        "g_ff2": 1.0 + nrm(ks[22], (DEPTH, D_MODEL), 0.1),
        "w_ff2_in": nrm(ks[23], (DEPTH, D_MODEL, 2 * D_FF), D_MODEL ** -0.5),
        "w_ff2_out": nrm(ks[24], (DEPTH, D_FF, D_MODEL), D_FF ** -0.5),
        "g_final": 1.0 + nrm(ks[25], (D_MODEL,), 0.1),
    }


def reference(x_prompt, x_sample, cache_k, cache_v, c, c_ctx, w_ada, b_ada, g_ff1, w_ff1_in,
              w_ff1_out, g_mix, w_in, attn_sink, w_attn_o, conv_a_w, conv_a_b, g_conv_a, w_a_out,
              conv_b_w, w_b_out, w_out, g_ff2, w_ff2_in, w_ff2_out, g_final):
    rows = x_sample.shape[1] // GRID_W
    xp = x_prompt
    xs = x_sample
    new_k = []
    new_v = []
    for l in range(DEPTH):
        mp = _ada(c_ctx[None, :], w_ada[l], b_ada[l])
        ms = _ada(c, w_ada[l], b_ada[l])
        merge_w = (w_attn_o[l], conv_a_w[l], conv_a_b[l], g_conv_a[l], w_a_out[l],
                   conv_b_w[l], w_b_out[l], w_out[l])
        xp = _ffn_half(xp, mp[0], mp[1], mp[2], g_ff1[l], w_ff1_in[l], w_ff1_out[l])
        q, k, v, a_in, bcx, gates = _mix_project(xp, mp[3], mp[4], g_mix[l], w_in[l])
        attn = _context_attention(q, k, v, attn_sink[l])
        xp = _mix_merge(xp, mp[5], attn, a_in, bcx, gates, *merge_w)
        xp = _ffn_half(xp, mp[6], mp[7], mp[8], g_ff2[l], w_ff2_in[l], w_ff2_out[l])
        new_k.append(k)
        new_v.append(v)
        xs = _ffn_half(xs, ms[0], ms[1], ms[2], g_ff1[l], w_ff1_in[l], w_ff1_out[l])
        q, k, v, a_in, bcx, gates = _mix_project(xs, ms[3], ms[4], g_mix[l], w_in[l])
        q = _rope_2d(q, rows)
        k = _rope_2d(k, rows)
        attn = _latent_attention(q, k, v, cache_k[:, l], cache_v[:, l], attn_sink[l])
        xs = _mix_merge(xs, ms[5], attn, a_in, bcx, gates, *merge_w)
        xs = _ffn_half(xs, ms[6], ms[7], ms[8], g_ff2[l], w_ff2_in[l], w_ff2_out[l])
    y_prompt = _rms_norm(xp, g_final)
    y_sample = _rms_norm(xs, g_final)
    new_cache_k = jnp.stack(new_k, axis=1)
    new_cache_v = jnp.stack(new_v, axis=1)
    return (y_prompt, y_sample, new_cache_k, new_cache_v)
```

```python
import numpy as np
from contextlib import ExitStack
import concourse.bass as bass
import concourse.mybir as mybir
from concourse.bass_utils import run_bass_kernel_spmd

F32 = mybir.dt.float32
BF16 = mybir.dt.bfloat16
U8 = mybir.dt.uint8
AF = mybir.ActivationFunctionType
ALU = mybir.AluOpType

D = 2048
KC = 16
DFF = 5632
NL = 2
TS = 512
NTOK = 2048
REG = 1536
NCORES = 8
KOFF = 640
KL = KOFF + REG + 128
AOFF = 560
AL = AOFF + REG + 16
APOS = (16, 288)
EPS = 1e-6
NEG = -30000.0
SCALE = 128.0 ** -0.5
ARENA = 210944
NSLOT = 6
SLOT = 8192


class Ev:
    __slots__ = ("sem", "val")

    def __init__(self, sem, val):
        self.sem = sem
        self.val = val


class Res:
    __slots__ = ("w", "r", "const", "excl")

    def __init__(self, const=False, excl=False):
        self.w = None
        self.r = {}
        self.const = const
        self.excl = excl

    def add_reader(self, ev):
        if self.const:
            return
        k = id(ev.sem)
        o = self.r.get(k)
        if o is None or o.val < ev.val:
            self.r[k] = ev

    def set_writer(self, ev):
        self.w = ev
        self.r = {}


class Eng:
    def __init__(self, nc, es, name, eng):
        self.name = name
        self.eng = eng
        self.sem = es.enter_context(nc.semaphore("p_" + name))
        self.cnt = 0
        self.waited = {}

    def wait_ev(self, ev):
        if ev is None or ev.sem is self.sem:
            return
        k = id(ev.sem)
        if self.waited.get(k, 0) >= ev.val:
            return
        self.eng.wait_ge(ev.sem, ev.val)
        self.waited[k] = ev.val

    def deps(self, reads, writes):
        best = {}

        def add(ev):
            if ev is None or ev.sem is self.sem:
                return
            k = id(ev.sem)
            o = best.get(k)
            if o is None or o.val < ev.val:
                best[k] = ev

        for R in reads:
            add(R.w)
            if R.excl:
                for e in R.r.values():
                    add(e)
        for R in writes:
            add(R.w)
            for e in R.r.values():
                add(e)
        for ev in best.values():
            self.wait_ev(ev)

    def done(self, ins, reads, writes):
        self.cnt += 1
        ins.then_inc(self.sem, 1)
        ev = Ev(self.sem, self.cnt)
        for R in reads:
            R.add_reader(ev)
        for R in writes:
            R.set_writer(ev)
        return ev

    def last(self):
        return Ev(self.sem, self.cnt) if self.cnt else None


class DmaQ:
    def __init__(self, nc, es, E, nsem, name):
        self.E = E
        self.sems = [es.enter_context(nc.semaphore("%s%d" % (name, i))) for i in range(nsem)]
        self.vals = [0] * nsem
        self.i = 0

    def dma(self, out, in_, reads, writes):
        E = self.E
        E.deps(reads, writes)
        k = self.i % len(self.sems)
        self.i += 1
        if self.vals[k]:
            E.wait_ev(Ev(self.sems[k], self.vals[k]))
        self.vals[k] += 16
        E.eng.dma_start(out=out, in_=in_).then_inc(self.sems[k], 16)
        ev = Ev(self.sems[k], self.vals[k])
        for R in reads:
            R.add_reader(ev)
        for R in writes:
            R.set_writer(ev)
        return ev

    def all_last(self):
        return [Ev(s, v) for s, v in zip(self.sems, self.vals) if v]


STOP = [None]
SKIP = set()


class _Stop(Exception):
    pass


def build_program():
    nc = bass.Bass("TRN2", target_bir_lowering=False)
    es = ExitStack()

    def din(name, shape):
        if "now" in SKIP and name.startswith("w_"):
            return nc.dram_tensor(name, [1] + list(shape)[1:], F32, kind="Internal").ap()
        return nc.dram_tensor(name, list(shape), F32, kind="ExternalInput").ap()

    def dout(name, shape):
        return nc.dram_tensor(name, list(shape), F32, kind="ExternalOutput").ap()

    def dscr(name, shape, dt):
        return nc.dram_tensor(name, list(shape), dt, kind="Internal").ap()

    xT_d = din("xT", [128, KC, NTOK])
    cT_d = din("cT", [128, KC, 2])
    w_ada_d = din("w_ada", [NL, D, 9 * D])
    badaT_d = din("badaT", [128, NL, 144])
    gT_d = din("gT", [128, NL, 3, KC])
    gfin_d = din("gfin", [128, KC])
    wffin_d = [din("w_ff1_in", [NL, D, 2 * DFF]), din("w_ff2_in", [NL, D, 2 * DFF])]
    wffout_d = [din("w_ff1_out", [NL, DFF, D]), din("w_ff2_out", [NL, DFF, D])]
    w_in_d = din("w_in", [NL, D, 14336])
    w_ao_d = din("w_attn_o", [NL, D, D])
    w_aout_d = din("w_a_out", [NL, 1024, D])
    w_bout_d = din("w_b_out", [NL, 1024, D])
    w_out_d = din("w_out", [NL, D, D])
    caw_d = din("caw", [128, NL, 8, 31])
    cab_d = din("cab", [128, NL, 8])
    gca_d = din("gca", [128, NL, 8])
    cbw_d = din("cbw", [128, NL, 8, 3])
    sinkb_d = din("sinkb", [128, NL, 16])
    kctxT_d = din("kctxT", [NL, 128, 4, 512])
    vctx_d = din("vctx", [NL, 512, 512])
    cosT_d = din("cosT", [128, REG])
    sinT_d = din("sinT", [128, REG])
    valid_d = din("valid", [128, REG])
    keybias_d = din("keybias", [128, 12])
    onesm_d = din("onesm", [128, 128])
    Rt_d = din("Rt", [128, 128])
    tri_d = din("tri", [128, 2, 512])

    yT_o = dout("yT", [128, KC, 1536])
    kTo_o = dout("kTo", [NL, 128, 4, 512])
    vo_o = dout("vo", [NL, 512, 512])

    xs_s = dscr("xs", [128, KC, NTOK], F32)
    ksp_s = dscr("ksp", [128, 4, KL], BF16)
    vsp_s = dscr("vsp", [KL, 512], BF16)
    asp_s = dscr("asp", [128, 8, AL], BF16)
    cxsp_s = dscr("cxsp", [128, 8, AL], BF16)
    bgsp_s = dscr("bgsp", [128, 8, NTOK], BF16)

    def kview(w2d):
        return w2d.rearrange("(kc p) n -> p kc n", p=128)

    PE = Eng(nc, es, "pe", nc.tensor)
    ACT = Eng(nc, es, "act", nc.scalar)
    DVE = Eng(nc, es, "dve", nc.vector)
    GP = Eng(nc, es, "gp", nc.gpsimd)
    SP = Eng(nc, es, "sp", nc.sync)
    SQ = DmaQ(nc, es, SP, 8, "sq")

    arena = es.enter_context(nc.sbuf_tensor("arena", [128, ARENA], U8))
    off = [0]

    def carve_at(o, shape, dt):
        nb = 4 if dt == F32 else 2
        n = int(np.prod(shape)) * nb
        v = arena[:, o:o + n].bitcast(dt)
        if len(shape) == 2:
            v = v.rearrange("p (a b) -> p a b", a=shape[0])
        elif len(shape) == 3:
            v = v.rearrange("p (a b c) -> p a b c", a=shape[0], b=shape[1])
        return v, o + ((n + 63) // 64) * 64

    def carve(shape, dt):
        v, off[0] = carve_at(off[0], shape, dt)
        return v

    ringb = carve([NSLOT, SLOT // 2], BF16)
    modsb = carve([NL, 144, 2], F32)
    Atab = carve([NL * 3 * KC * 2], F32)
    Gtab = carve([NL * 3 * KC * 2], F32)
    badaT = carve([NL, 144], F32)
    gT = carve([NL * 3, KC], F32)
    gfin = carve([KC], F32)
    caw = carve([NL * 8, 31], F32)
    cab = carve([NL * 8], F32)
    gca = carve([NL * 8], F32)
    cbw = carve([NL * 8, 3], F32)
    sinkb = carve([NL * 16], F32)
    sexp = carve([NL * 16], F32)
    cTf = carve([KC, 2], F32)
    csil = carve([KC, 2], BF16)
    ones = carve([128], BF16)
    Rt = carve([128], BF16)
    tri = carve([2, 512], BF16)
    keybias = carve([12], F32)
    valid = carve([REG], BF16)
    cosb = [carve([512], F32) for _ in range(2)]
    sinb = [carve([512], F32) for _ in range(2)]
    sqb = [carve([512], BF16) for _ in range(3)]
    rstd = carve([512], F32)
    ntmp = [carve([512], F32) for _ in range(3)]
    xb = [carve([KC, 512], F32), None]
    hb = [carve([KC, 512], BF16), None]
    PH0 = off[0]
    xb[1] = carve([KC, 512], F32)
    hb[1] = carve([KC, 512], BF16)
    PHT = off[0]
    assert ARENA - PHT >= 24576, (ARENA - PHT)
    o = PHT
    hid = []
    for i in range(2):
        v, o = carve_at(o, [2, 1024], BF16)
        hid.append(v)
    su = []
    for i in range(3):
        v, o = carve_at(o, [512], BF16)
        su.append(v)
    stb = []
    for i in range(4):
        v, o = carve_at(o, [512], BF16)
        stb.append(v)
    stf = []
    for i in range(2):
        v, o = carve_at(o, [512], F32)
        stf.append(v)
    rxb1, o = carve_at(o, [512], BF16)
    rt1a, o = carve_at(o, [512], F32)
    rt2a, o = carve_at(o, [512], F32)
    assert o <= ARENA, o
    o = PH0
    attn, o = carve_at(o, [KC, 512], BF16)
    afeat, o = carve_at(o, [8, 512], BF16)
    binp, o = carve_at(o, [8, 512], BF16)
    T0 = o
    awin, o = carve_at(T0, [8, 576], BF16)
    acv, o = carve_at(o, [8, 512], F32)
    assert o <= ARENA
    cxwin, o = carve_at(T0, [8, 544], BF16)
    bgwin, o = carve_at(o, [8, 512], BF16)
    cacc, o = carve_at(o, [512], F32)
    assert o <= ARENA
    qg, o = carve_at(T0, [4, 512], BF16)
    kwin, o = carve_at(o, [4, 768], BF16)
    vwin, o = carve_at(o, [6, 512], BF16)
    kctx, o = carve_at(o, [4, 512], BF16)
    vctx, o = carve_at(o, [4, 512], BF16)
    pT = []
    for i in range(3):
        v, o = carve_at(o, [512], BF16)
        pT.append(v)
    dtmp, o = carve_at(o, [512], F32)
    rxb2 = []
    rt1b = []
    rt2b = []
    for i in range(2):
        v, o = carve_at(o, [512], BF16)
        rxb2.append(v)
        v, o = carve_at(o, [512], F32)
        rt1b.append(v)
        v, o = carve_at(o, [512], F32)
        rt2b.append(v)
    assert o <= ARENA, o
    mixed, o = carve_at(T0, [KC, 512], BF16)
    sgm = []
    for i in range(3):
        v, o = carve_at(o, [512], BF16)
        sgm.append(v)
    mt = []
    for i in range(3):
        v, o = carve_at(o, [512], F32)
        mt.append(v)
    assert o <= ARENA, o

    PB = [es.enter_context(nc.psum_tensor("pb%d" % i, [128, 512], F32)) for i in range(8)]
    PBR = [Res(excl=True) for _ in range(8)]
    rot = [0]

    def nbank():
        i = rot[0] % 5
        rot[0] += 1
        return i

    def act(out, in_, func, reads, writes, bias=None, scale=None):
        ACT.deps(reads, writes)
        kw = {}
        if bias is not None:
            kw["bias"] = bias
        if scale is not None:
            kw["scale"] = scale
        ins = nc.scalar.activation(out=out, in_=in_, func=func, **kw)
        return ACT.done(ins, reads, writes)

    def dve(fn, reads, writes):
        DVE.deps(reads, writes)
        ins = fn(nc.vector)
        return DVE.done(ins, reads, writes)

    def mm(specs, reads, writes):
        PE.deps(reads, writes)
        ins = None
        for (o_, l_, r_, st_, sp_) in specs:
            ins = nc.tensor.matmul(o_, lhsT=l_, rhs=r_, start=st_, stop=sp_)
        return PE.done(ins, reads, writes)

    def barrier():
        evs = [PE.last(), ACT.last(), DVE.last()] + SQ.all_last()
        for E in (PE, ACT, DVE, SP):
            for e in evs:
                E.wait_ev(e)

    ring_sems = [es.enter_context(nc.semaphore("ring%d" % i)) for i in range(NSLOT)]
    ring_vals = [0] * NSLOT
    ring_res = [Res() for _ in range(NSLOT)]
    ring_n = [0]

    def wload(parts, shape):
        s = ring_n[0] % NSLOT
        ring_n[0] += 1
        R = ring_res[s]
        GP.deps([], [R])
        n = int(np.prod(shape))
        v = ringb[:, s, 0:n]
        if len(shape) == 2:
            v = v.rearrange("p (a b) -> p a b", a=shape[0])
        for dstf, src in parts:
            nc.gpsimd.dma_start(out=dstf(v), in_=src).then_inc(ring_sems[s], 16)
            ring_vals[s] += 16
        R.set_writer(Ev(ring_sems[s], ring_vals[s]))
        return v, R

    def wload1(src, shape):
        return wload([(lambda v: v, src)], shape)

    def gload(dst, src, R):
        raise NotImplementedError

    setup_sem = [es.enter_context(nc.semaphore("setup_hw")), es.enter_context(nc.semaphore("setup_sw"))]
    setup_n = [0, 0]

    def sload(dst, src, cast=False):
        eng = nc.gpsimd if cast else nc.sync
        i = 1 if cast else 0
        eng.dma_start(out=dst, in_=src).then_inc(setup_sem[i], 16)
        setup_n[i] += 16

    sload(badaT, badaT_d)
    sload(gT, gT_d.rearrange("p l s k -> p (l s) k"))
    sload(gfin, gfin_d)
    sload(caw, caw_d.rearrange("p l c j -> p (l c) j"))
    sload(cab, cab_d.rearrange("p l c -> p (l c)"))
    sload(gca, gca_d.rearrange("p l c -> p (l c)"))
    sload(cbw, cbw_d.rearrange("p l c j -> p (l c) j"))
    sload(sinkb, sinkb_d.rearrange("p l h -> p (l h)"))
    sload(cTf, cT_d)
    sload(keybias, keybias_d)
    sload(ones, onesm_d, cast=True)
    sload(Rt, Rt_d, cast=True)
    sload(tri, tri_d, cast=True)
    sload(valid, valid_d, cast=True)
    setup_evs = [Ev(setup_sem[0], setup_n[0]), Ev(setup_sem[1], setup_n[1])]
    R_zero = Res()
    dve(lambda v: v.memset(stb[0][:, 0:128], 0.0), [], [R_zero])
    R_asp = Res()
    R_cxsp = Res()
    zsrc = stb[0][:, 0:128].rearrange("p (c j) -> p c j", c=8)
    for p0 in (0, 272, 544):
        SQ.dma(asp_s[:, :, p0:p0 + 16], zsrc, [R_zero], [])
        SQ.dma(cxsp_s[:, :, p0:p0 + 16], zsrc, [R_zero], [])
    for e in SQ.all_last():
        R_asp.add_reader(e)
    R_attn = Res()
    dve(lambda v: v.memset(attn[:, :, :], 0.0), [], [R_attn])
    for E in (PE, ACT, DVE, SP):
        for e_ in setup_evs:
            E.wait_ev(e_)
    barrier()

    RC = Res(const=True)

    R_csil = Res()
    act(csil[:, :, :], cTf[:, :, :], AF.Silu, [RC], [R_csil])
    R_mods = Res()
    for l in range(NL):
        wv = kview(w_ada_d[l if 'now' not in SKIP else 0])
        for s2 in range(72 if "ada" not in SKIP else 1):
            slab, R = wload1(wv[:, :, s2 * 256:(s2 + 1) * 256], [KC, 256])
            specs = []
            for j in range(2):
                n = s2 * 2 + j
                for kc in range(KC):
                    specs.append((PB[7][:, 2 * n:2 * n + 2], slab[:, kc, j * 128:(j + 1) * 128],
                                  csil[:, kc, :], kc == 0, kc == KC - 1))
            mm(specs, [R, R_csil], [PBR[7]])
        pv = PB[7][:, 0:288].rearrange("p (n v) -> p n v", v=2)
        for v_ in range(2):
            dve(lambda v, v_=v_: v.tensor_tensor(out=modsb[:, l, :, v_], in0=pv[:, :, v_], in1=badaT[:, l, :],
                                                 op=ALU.add), [PBR[7]], [R_mods])
    A3 = Atab.rearrange("p (q k v) -> p q k v", k=KC, v=2)
    G3 = Gtab.rearrange("p (q k v) -> p q k v", k=KC, v=2)
    for l in range(NL):
        for s_ in range(3):
            for v_ in range(2):
                dve(lambda v, l=l, s_=s_, v_=v_: v.scalar_tensor_tensor(
                    out=A3[:, l * 3 + s_, :, v_], in0=modsb[:, l, (3 * s_ + 1) * 16:(3 * s_ + 2) * 16, v_],
                    scalar=1.0, in1=gT[:, l * 3 + s_, :], op0=ALU.add, op1=ALU.mult), [R_mods], [R_mods])
                dve(lambda v, l=l, s_=s_, v_=v_: v.tensor_scalar(
                    out=G3[:, l * 3 + s_, :, v_], in0=modsb[:, l, (3 * s_ + 2) * 16:(3 * s_ + 3) * 16, v_],
                    scalar1=(1.0 if s_ == 1 else 0.5), scalar2=None, op0=ALU.mult), [R_mods], [R_mods])
    act(sexp[:, :], sinkb[:, :], AF.Exp, [RC], [R_mods])
    for E in (PE, ACT, DVE, SP):
        E.wait_ev(R_mods.w)
    def Asc(l, s_, kc, v_):
        i = ((l * 3 + s_) * KC + kc) * 2 + v_
        return Atab[:, i:i + 1]

    def Gsc(l, s_, kc, v_):
        i = ((l * 3 + s_) * KC + kc) * 2 + v_
        return Gtab[:, i:i + 1]

    def Bsc(l, s_, kc, v_):
        return modsb[:, l, 3 * s_ * 16 + kc, v_:v_ + 1]

    R_x = [[Res() for _ in range(KC)] for _ in range(2)]
    R_h = [[Res() for _ in range(KC)] for _ in range(2)]
    R_sq = [Res() for _ in range(3)]
    R_rstd = Res()
    R_nt = [Res() for _ in range(3)]
    R_hid = [[[Res() for _ in range(2)] for _ in range(2)] for _ in range(2)]
    R_su = [Res() for _ in range(3)]
    R_stb = [Res() for _ in range(4)]
    R_stf = [Res() for _ in range(2)]
    R_cs = [Res() for _ in range(2)]
    R_xs = [Res() for _ in range(4)]
    R_ksp = [Res() for _ in range(4)]
    R_vsp = [Res() for _ in range(4)]
    R_aspt = [Res() for _ in range(4)]
    R_cxt = [Res() for _ in range(4)]
    R_bgt = [Res() for _ in range(4)]
    R_out = Res()
    cnt = {"sq": 0, "nt": 0, "su": 0, "stb": 0, "stf": 0}

    def rr(name, n):
        i = cnt[name] % n
        cnt[name] += 1
        return i

    def tile_vec(tile):
        return 0 if tile == 0 else 1

    def rstd_from_bank7(nparts):
        dve(lambda v: v.tensor_scalar(out=rstd[:, :], in0=PB[7][:, :], scalar1=1.0 / nparts, scalar2=EPS,
                                      op0=ALU.mult, op1=ALU.add), [PBR[7]], [R_rstd])
        act(rstd[:, :], rstd[:, :], AF.Sqrt, [R_rstd], [R_rstd])
        dve(lambda v: v.reciprocal(out=rstd[:, :], in_=rstd[:, :]), [R_rstd], [R_rstd])

    def sumsq(srcs):
        n = len(srcs)
        for i, (ap, R) in enumerate(srcs):
            b = rr("sq", 3)
            act(sqb[b][:, :], ap, AF.Square, [R], [R_sq[b]])
            mm([(PB[7][:, :], ones[:, :], sqb[b][:, :], i == 0, i == n - 1)], [R_sq[b]], [PBR[7]])

    def norm_mod(l, s_, t, v_):
        sumsq([(xb[t][:, kc, :], R_x[t][kc]) for kc in range(KC)])
        rstd_from_bank7(D)
        for kc in range(KC):
            b = rr("nt", 3)
            dve(lambda v, kc=kc, b=b: v.scalar_tensor_tensor(
                out=ntmp[b][:, :], in0=xb[t][:, kc, :], scalar=Asc(l, s_, kc, v_), in1=rstd[:, :],
                op0=ALU.mult, op1=ALU.mult), [R_x[t][kc], R_rstd], [R_nt[b]])
            act(hb[t][:, kc, :], ntmp[b][:, :], AF.Identity, [R_nt[b]], [R_h[t][kc]],
                bias=Bsc(l, s_, kc, v_), scale=1.0)

    def ffn(l, which, tiles):
        if "ffn" in SKIP:
            return
        s_ = 0 if which == 0 else 2
        nt_ = len(tiles)
        for t in range(nt_):
            norm_mod(l, s_, t, tile_vec(tiles[t]))
        win = kview(wffin_d[which][l])
        wout = kview(wffout_d[which][l])
        allh = [R_h[t][kc] for t in range(nt_) for kc in range(KC)]

        def in_proj(g):
            hbuf = g % 2
            slabU, RU = wload1(win[:, :, g * 256:(g + 1) * 256], [KC, 256])
            slabV, RV = wload1(win[:, :, DFF + g * 256:DFF + (g + 1) * 256], [KC, 256])
            for j in range(2):
                for t in range(nt_):
                    bu = nbank()
                    mm([(PB[bu][:, :], slabU[:, kc, j * 128:(j + 1) * 128], hb[t][:, kc, :], kc == 0, kc == KC - 1)
                        for kc in range(KC)], [RU] + [R_h[t][kc] for kc in range(KC)], [PBR[bu]])
                    si = rr("su", 3)
                    act(su[si][:, :], PB[bu][:, :], AF.Silu, [PBR[bu]], [R_su[si]])
                    bv = nbank()
                    mm([(PB[bv][:, :], slabV[:, kc, j * 128:(j + 1) * 128], hb[t][:, kc, :], kc == 0, kc == KC - 1)
                        for kc in range(KC)], [RV] + [R_h[t][kc] for kc in range(KC)], [PBR[bv]])
                    dve(lambda v, bv=bv, si=si, j=j, t=t: v.tensor_tensor(
                        out=hid[hbuf][:, j, t * 512:(t + 1) * 512], in0=PB[bv][:, :], in1=su[si][:, :],
                        op=ALU.mult), [PBR[bv], R_su[si]], [R_hid[hbuf][j][t]])

        def out_proj(g):
            hbuf = g % 2
            slabO, RO = wload1(wout[:, 2 * g:2 * g + 2, :], [2, 2048])
            for oc in range(KC):
                for t in range(nt_):
                    b = nbank()
                    mm([(PB[b][:, :], slabO[:, j, oc * 128:(oc + 1) * 128], hid[hbuf][:, j, t * 512:(t + 1) * 512],
                         j == 0, j == 1) for j in range(2)],
                       [RO, R_hid[hbuf][0][t], R_hid[hbuf][1][t]], [PBR[b]])
                    dve(lambda v, b=b, oc=oc, t=t: v.scalar_tensor_tensor(
                        out=xb[t][:, oc, :], in0=PB[b][:, :], scalar=Gsc(l, s_, oc, tile_vec(tiles[t])),
                        in1=xb[t][:, oc, :], op0=ALU.mult, op1=ALU.add), [PBR[b]], [R_x[t][oc]])

        NG = DFF // 256
        in_proj(0)
        for g in range(NG):
            if g + 1 < NG:
                in_proj(g + 1)
            out_proj(g)

    def rope(bank, cs, out_ap, R_outs, xbuf_, t1_, t2_, Rtmp):
        Rxb, Rt1, Rt2 = Rtmp
        if "altdst" in SKIP:
            act(stb[3][:, :], PB[bank][:, :], AF.Copy, [PBR[bank]], [Rxb])
        elif "noact" not in SKIP:
            act(xbuf_[:, :], PB[bank][:, :], AF.Copy, [PBR[bank]], [Rxb])
        if "norot" in SKIP:
            b2 = bank
        else:
            b2 = nbank()
            mm([(PB[b2][:, :], Rt[:, :], xbuf_[:, :], True, True)], [Rxb], [PBR[b2]])
        dve(lambda v: v.tensor_tensor(out=t1_[:, :], in0=PB[bank][:, :], in1=cosb[cs][:, :], op=ALU.mult),
            [PBR[bank], R_cs[cs]] + ([Rxb] if "seq" in SKIP else []), [Rt1])
        dve(lambda v: v.tensor_tensor(out=t2_[:, :], in0=PB[b2][:, :], in1=sinb[cs][:, :], op=ALU.mult),
            [PBR[b2], R_cs[cs]], [Rt2])
        dve(lambda v: v.tensor_tensor(out=out_ap, in0=t1_[:, :], in1=t2_[:, :], op=ALU.add),
            [Rt1, Rt2], R_outs)

    def load_cs(cs, tile):
        rp0 = (tile - 1) * 512
        SQ.dma(cosb[cs][:, :], cosT_d[:, rp0:rp0 + 512], [], [R_cs[cs]])
        SQ.dma(sinb[cs][:, :], sinT_d[:, rp0:rp0 + 512], [], [R_cs[cs]])

    R_rm1 = (Res(), Res(), Res())

    def m1(l, tiles):
        nt_ = len(tiles)
        wv = kview(w_in_d[l])
        for t in range(nt_):
            norm_mod(l, 1, t, tile_vec(tiles[t]))
            if tiles[t] > 0:
                load_cs(t, tiles[t])
        hall = [[R_h[t][kc] for kc in range(KC)] for t in range(nt_)]

        def proj(slab, R, j, t):
            b = nbank()
            mm([(PB[b][:, :], slab[:, kc, j * 128:(j + 1) * 128], hb[t][:, kc, :], kc == 0, kc == KC - 1)
                for kc in range(KC)], [R] + hall[t], [PBR[b]])
            return b

        for hp in range(2):
            slab, R = wload1(wv[:, :, 2048 + hp * 256:2048 + (hp + 1) * 256], [KC, 256])
            for j in range(2):
                head = 2 * hp + j
                for t in range(nt_):
                    tile = tiles[t]
                    b = proj(slab, R, j, t)
                    si = rr("stb", 4)
                    if tile == 0:
                        act(stb[si][:, :], PB[b][:, :], AF.Copy, [PBR[b]], [R_stb[si]])
                        fi = rr("stf", 2)
                        dve(lambda v, b=b, fi=fi: v.tensor_copy(out=stf[fi][:, :], in_=PB[b][:, :]),
                            [PBR[b]], [R_stf[fi]])
                        SQ.dma(kTo_o[l, :, head, :], stf[fi][:, :], [R_stf[fi]], [])
                        SQ.dma(ksp_s[:, head, 0:512], stb[si][:, :], [R_stb[si]], [])
                        stop('m1k0')
                    else:
                        rope(b, t, stb[si][:, :], [R_stb[si]], rxb1, rt1a, rt2a, R_rm1)
                        p0 = KOFF + (tile - 1) * 512
                        SQ.dma(ksp_s[:, head, p0:p0 + 512], stb[si][:, :], [R_stb[si]], [])
                        stop('m1k1')
        stop('m1k')
        for half in range(2):
            slab, R = wload1(wv[:, :, 2560 + half * 256:2560 + (half + 1) * 256], [KC, 256])
            for t in range(nt_):
                tile = tiles[t]
                for blk in range(4):
                    b = nbank()
                    mm([(PB[b][:, 0:256], hb[t][:, kc, blk * 128:(blk + 1) * 128], slab[:, kc, :], kc == 0,
                         kc == KC - 1) for kc in range(KC)], [R] + hall[t], [PBR[b]])
                    si = rr("stb", 4)
                    act(stb[si][:, 0:256], PB[b][:, 0:256], AF.Copy, [PBR[b]], [R_stb[si]])
                    if tile == 0:
                        r0 = blk * 128
                        fi = rr("stf", 2)
                        dve(lambda v, b=b, fi=fi: v.tensor_copy(out=stf[fi][:, 0:256], in_=PB[b][:, 0:256]),
                            [PBR[b]], [R_stf[fi]])
                        SQ.dma(vo_o[l, r0:r0 + 128, half * 256:(half + 1) * 256], stf[fi][:, 0:256],
                               [R_stf[fi]], [])
                    else:
                        r0 = KOFF + (tile - 1) * 512 + blk * 128
                    SQ.dma(vsp_s[r0:r0 + 128, half * 256:(half + 1) * 256], stb[si][:, 0:256], [R_stb[si]],
                           [])

        stop('m1v')

        def conv_store(scr, Rscr, c, tile, si):
            if tile == 0:
                for sg_ in range(2):
                    p0 = APOS[sg_]
                    SQ.dma(scr[:, c, p0:p0 + 256], stb[si][:, sg_ * 256:(sg_ + 1) * 256], [R_stb[si]], [])
            else:
                p0 = AOFF + (tile - 1) * 512
                SQ.dma(scr[:, c, p0:p0 + 512], stb[si][:, :], [R_stb[si]], [])

        for cp in range(4):
            slabA, RA = wload1(wv[:, :, 3072 + cp * 256:3072 + (cp + 1) * 256], [KC, 256])
            slabG, RG = wload1(wv[:, :, 4096 + cp * 256:4096 + (cp + 1) * 256], [KC, 256])
            for j in range(2):
                c = 2 * cp + j
                for t in range(nt_):
                    tile = tiles[t]
                    bg_ = proj(slabG, RG, j, t)
                    ui = rr("su", 3)
                    act(su[ui][:, :], PB[bg_][:, :], AF.Sigmoid, [PBR[bg_]], [R_su[ui]])
                    if tile > 0:
                        rp0 = (tile - 1) * 512
                        dve(lambda v, ui=ui, rp0=rp0: v.tensor_tensor(out=su[ui][:, :], in0=su[ui][:, :],
                                                                      in1=valid[:, rp0:rp0 + 512], op=ALU.mult),
                            [R_su[ui]], [R_su[ui]])
                    ba = proj(slabA, RA, j, t)
                    si = rr("stb", 4)
                    dve(lambda v, ba=ba, ui=ui, si=si: v.tensor_tensor(out=stb[si][:, :], in0=PB[ba][:, :],
                                                                       in1=su[ui][:, :], op=ALU.mult),
                        [PBR[ba], R_su[ui]], [R_stb[si]])
                    conv_store(asp_s, R_aspt, c, tile, si)
        stop('m1a')
        for cp in range(4):
            slabB, RB = wload1(wv[:, :, 5120 + cp * 256:5120 + (cp + 1) * 256], [KC, 256])
            slabC, RCg = wload1(wv[:, :, 6144 + cp * 256:6144 + (cp + 1) * 256], [KC, 256])
            slabX, RX = wload1(wv[:, :, 7168 + cp * 256:7168 + (cp + 1) * 256], [KC, 256])
            for j in range(2):
                c = 2 * cp + j
                for t in range(nt_):
                    tile = tiles[t]
                    bb = proj(slabB, RB, j, t)
                    si = rr("stb", 4)
                    act(stb[si][:, :], PB[bb][:, :], AF.Copy, [PBR[bb]], [R_stb[si]])
                    SQ.dma(bgsp_s[:, c, tile * 512:(tile + 1) * 512], stb[si][:, :], [R_stb[si]], [])
                    bx = proj(slabX, RX, j, t)
                    ui = rr("su", 3)
                    act(su[ui][:, :], PB[bx][:, :], AF.Copy, [PBR[bx]], [R_su[ui]])
                    if tile > 0:
                        rp0 = (tile - 1) * 512
                        dve(lambda v, ui=ui, rp0=rp0: v.tensor_tensor(out=su[ui][:, :], in0=su[ui][:, :],
                                                                      in1=valid[:, rp0:rp0 + 512], op=ALU.mult),
                            [R_su[ui]], [R_su[ui]])
                    bc = proj(slabC, RCg, j, t)
                    si = rr("stb", 4)
                    dve(lambda v, bc=bc, ui=ui, si=si: v.tensor_tensor(out=stb[si][:, :], in0=PB[bc][:, :],
                                                                       in1=su[ui][:, :], op=ALU.mult),
                        [PBR[bc], R_su[ui]], [R_stb[si]])
                    conv_store(cxsp_s, R_cxt, c, tile, si)

    def load_x(src, t, c0):
        SQ.dma(xb[t][:, :, :], src[:, :, c0:c0 + 512], [],
               [R_x[t][kc] for kc in range(KC)])

    def store_x(t, tile):
        SQ.dma(xs_s[:, :, tile * 512:(tile + 1) * 512], xb[t][:, :, :], [R_x[t][kc] for kc in range(KC)],
               [])

    R_T = Res()

    def ph2(l, tile):
        sample = tile > 0
        v_ = tile_vec(tile)
        wv = kview(w_in_d[l])
        barrier()
        load_x(xs_s, 0, tile * 512)
        if sample:
            load_cs(0, tile)
        norm_mod(l, 1, 0, v_)
        hall = [R_h[0][kc] for kc in range(KC)]
        rp0 = (tile - 1) * 512
        if sample:
            segs = [(0, 512, 0)]
        else:
            segs = [(0, 256, 0), (256, 256, 288)]
        R_aw = Res()
        if sample:
            SQ.dma(awin[:, :, 0:542], asp_s[:, :, AOFF + rp0 - 15:AOFF + rp0 + 527],
                   [], [R_aw])
        else:
            for sg_ in range(2):
                p0 = APOS[sg_] - 15
                SQ.dma(awin[:, :, sg_ * 288:sg_ * 288 + 286], asp_s[:, :, p0:p0 + 286], [], [R_aw])
        R_acv = [Res() for _ in range(8)]
        for c in range(8):
            for (o0, ln, w0) in segs:
                for j in range(31):
                    if j == 0:
                        dve(lambda v, c=c, o0=o0, ln=ln, w0=w0: v.tensor_scalar(
                            out=acv[:, c, o0:o0 + ln], in0=awin[:, c, w0:w0 + ln], scalar1=caw[:, l * 8 + c, 0:1],
                            scalar2=cab[:, l * 8 + c:l * 8 + c + 1], op0=ALU.mult, op1=ALU.add), [R_aw], [R_acv[c]])
                    else:
                        dve(lambda v, c=c, o0=o0, ln=ln, w0=w0, j=j: v.scalar_tensor_tensor(
                            out=acv[:, c, o0:o0 + ln], in0=awin[:, c, w0 + j:w0 + j + ln],
                            scalar=caw[:, l * 8 + c, j:j + 1], in1=acv[:, c, o0:o0 + ln], op0=ALU.mult,
                            op1=ALU.add), [R_aw], [R_acv[c]])
        sumsq([(acv[:, c, :], R_acv[c]) for c in range(8)])
        rstd_from_bank7(1024)
        R_af = [Res() for _ in range(8)]
        for c in range(8):
            b = rr("nt", 3)
            dve(lambda v, c=c, b=b: v.scalar_tensor_tensor(
                out=ntmp[b][:, :], in0=acv[:, c, :], scalar=gca[:, l * 8 + c:l * 8 + c + 1], in1=rstd[:, :],
                op0=ALU.mult, op1=ALU.mult), [R_acv[c], R_rstd], [R_nt[b]])
            act(afeat[:, c, :], ntmp[b][:, :], AF.Silu, [R_nt[b]], [R_af[c]])
        barrier()
        R_cw = Res()
        R_bw = Res()
        if sample:
            SQ.dma(cxwin[:, :, 0:514], cxsp_s[:, :, AOFF + rp0 - 1:AOFF + rp0 + 513],
                   [], [R_cw])
            bsegs = [(0, 512, 0)]
        else:
            for sg_ in range(2):
                p0 = APOS[sg_] - 1
                SQ.dma(cxwin[:, :, sg_ * 272:sg_ * 272 + 258], cxsp_s[:, :, p0:p0 + 258], [], [R_cw])
            bsegs = [(0, 256, 0), (256, 256, 272)]
        SQ.dma(bgwin[:, :, :], bgsp_s[:, :, tile * 512:(tile + 1) * 512], [], [R_bw])
        R_ca = Res()
        R_bi = [Res() for _ in range(8)]
        for c in range(8):
            for (o0, ln, w0) in bsegs:
                dve(lambda v, c=c, o0=o0, ln=ln, w0=w0: v.tensor_scalar(
                    out=cacc[:, o0:o0 + ln], in0=cxwin[:, c, w0:w0 + ln], scalar1=cbw[:, l * 8 + c, 0:1],
                    scalar2=None, op0=ALU.mult), [R_cw], [R_ca])
                for j in (1, 2):
                    dve(lambda v, c=c, o0=o0, ln=ln, w0=w0, j=j: v.scalar_tensor_tensor(
                        out=cacc[:, o0:o0 + ln], in0=cxwin[:, c, w0 + j:w0 + j + ln],
                        scalar=cbw[:, l * 8 + c, j:j + 1], in1=cacc[:, o0:o0 + ln], op0=ALU.mult, op1=ALU.add),
                        [R_cw], [R_ca])
            dve(lambda v, c=c: v.tensor_tensor(out=binp[:, c, :], in0=cacc[:, :], in1=bgwin[:, c, :],
                                               op=ALU.mult), [R_ca, R_bw], [R_bi[c]])
        barrier()
        R_kw = Res()
        R_vw = Res()
        R_kc = Res()
        R_vc = Res()
        if sample:
            p0 = KOFF + rp0 - 128
            SQ.dma(kwin[:, :, :], ksp_s[:, :, p0:p0 + 768], [], [R_kw])
            SQ.dma(vwin[:, :, :], vsp_s[p0:p0 + 768, :].rearrange("(b p) d -> p b d", p=128),
                   [], [R_vw])
        else:
            SQ.dma(kwin[:, :, 0:512], ksp_s[:, :, 0:512], [], [R_kw])
            SQ.dma(vwin[:, 0:4, :], vsp_s[0:512, :].rearrange("(b p) d -> p b d", p=128), [], [R_vw])
        if sample:
            ctx_load(l, R_kc, R_vc)
        R_q = [Res() for _ in range(4)]
        R_pT = [Res() for _ in range(3)]
        R_dt = Res()
        R_rb = [(Res(), Res(), Res()) for _ in range(2)]
        pcount = 0
        if sample:
            blk0 = (tile - 1) * 4
            lo, hi = (1, 10) if l == 0 else (2, 9)
            qbs = [qb for qb in range(4) if lo <= blk0 + qb <= hi]
        else:
            qbs = [0, 1, 2, 3]
        for g in range(4):
            slabs = []
            for hh in range(2):
                slabs.append(wload1(wv[:, :, g * 512 + hh * 256:g * 512 + (hh + 1) * 256], [KC, 256]))
            for j in range(4):
                slab, R = slabs[j // 2]
                b = nbank()
                mm([(PB[b][:, :], slab[:, kc, (j % 2) * 128:(j % 2 + 1) * 128], hb[0][:, kc, :], kc == 0,
                     kc == KC - 1) for kc in range(KC)], [R] + hall, [PBR[b]])
                if sample:
                    ri = j % 2
                    rope(b, 0, qg[:, j, :], [R_q[j]], rxb2[ri], rt1b[ri], rt2b[ri], R_rb[ri])
                else:
                    act(qg[:, j, :], PB[b][:, :], AF.Copy, [PBR[b]], [R_q[j]])
            for qb in qbs:
                chunks = []
                if sample:
                    for w_ in range(3):
                        kb = blk0 + qb - 1 + w_
                        chunks.append((kwin[:, g, (qb + w_) * 128:(qb + w_ + 1) * 128],
                                       vwin[:, qb + w_, g * 128:(g + 1) * 128],
                                       keybias[:, kb:kb + 1] if 0 <= kb < 12 else None,
                                       (0 if w_ == 0 else (1 if w_ == 2 else None)), [R_kw, R_vw]))
                    for c_ in range(4):
                        chunks.append((kctx[:, g, c_ * 128:(c_ + 1) * 128], vctx[:, c_, g * 128:(g + 1) * 128],
                                       None, None, [R_kc, R_vc]))
                else:
                    sq_ = qb // 2
                    for c_ in range(2):
                        kk = sq_ * 2 + c_
                        chunks.append((kwin[:, g, kk * 128:(kk + 1) * 128], vwin[:, kk, g * 128:(g + 1) * 128],
                                       None, None, [R_kw, R_vw]))
                nch = len(chunks)
                rq = qg[:, :, qb * 128:(qb + 1) * 128]
                for ci, (kap, vap, bias, msk, rds) in enumerate(chunks):
                    b = nbank()
                    mm([(PB[b][:, :].rearrange("p (h q) -> p h q", h=4), kap, rq, True, True)], rds + R_q, [PBR[b]])
                    pi = pcount % 3
                    pcount += 1
                    act(pT[pi][:, :], PB[b][:, :], AF.Exp, [PBR[b]], [R_pT[pi]], bias=bias, scale=SCALE)
                    if msk is not None:
                        dve(lambda v, pi=pi, msk=msk: v.tensor_tensor(out=pT[pi][:, :], in0=pT[pi][:, :],
                                                                      in1=tri[:, msk, :], op=ALU.mult),
                            [R_pT[pi]], [R_pT[pi]])
                    mm([(PB[5][:, :], vap, pT[pi][:, :], ci == 0, ci == nch - 1)], rds + [R_pT[pi]], [PBR[5]])
                    mm([(PB[6][:, :], ones[:, :], pT[pi][:, :], ci == 0, ci == nch - 1)], [R_pT[pi]], [PBR[6]])
                for j in range(4):
                    h_ = l * 16 + 4 * g + j
                    dve(lambda v, j=j, h_=h_: v.tensor_scalar(out=dtmp[:, j * 128:(j + 1) * 128],
                                                              in0=PB[6][:, j * 128:(j + 1) * 128],
                                                              scalar1=sexp[:, h_:h_ + 1], scalar2=None, op0=ALU.add),
                        [PBR[6]], [R_dt])
                dve(lambda v: v.reciprocal(out=dtmp[:, :], in_=dtmp[:, :]), [R_dt], [R_dt])
                dve(lambda v, g=g, qb=qb: v.tensor_tensor(
                    out=attn[:, 4 * g:4 * g + 4, qb * 128:(qb + 1) * 128],
                    in0=PB[5][:, :].rearrange("p (h q) -> p h q", h=4),
                    in1=dtmp[:, :].rearrange("p (h q) -> p h q", h=4), op=ALU.mult), [PBR[5], R_dt], [R_attn])
        barrier()
        R_sg = [Res() for _ in range(3)]
        R_mt = [Res() for _ in range(3)]
        R_mx = [Res() for _ in range(KC)]
        wao = kview(w_ao_d[l])
        wa = kview(w_aout_d[l])
        wb = kview(w_bout_d[l])
        for op_ in range(8):
            c0 = op_ * 256
            slabO, RO = wload1(wao[:, :, c0:c0 + 256], [KC, 256])
            slabAB, RAB = wload([(lambda v: v[:, 0:8, :], wa[:, :, c0:c0 + 256]),
                                 (lambda v: v[:, 8:16, :], wb[:, :, c0:c0 + 256])], [KC, 256])
            slabG = [wload1(wv[:, :, 8192 + gi * 2048 + c0:8192 + gi * 2048 + c0 + 256], [KC, 256])
                     for gi in range(3)]
            for j in range(2):
                oc = op_ * 2 + j
                cs_ = slice(j * 128, (j + 1) * 128)
                for gi in range(3):
                    slab, R = slabG[gi]
                    b = nbank()
                    mm([(PB[b][:, :], slab[:, kc, cs_], hb[0][:, kc, :], kc == 0, kc == KC - 1) for kc in range(KC)],
                       [R] + hall, [PBR[b]])
                    act(sgm[gi][:, :], PB[b][:, :], AF.Sigmoid, [PBR[b]], [R_sg[gi]])
                    b2 = nbank()
                    if gi == 0:
                        mm([(PB[b2][:, :], slabAB[:, c, cs_], afeat[:, c, :], c == 0, c == 7) for c in range(8)],
                           [RAB] + R_af, [PBR[b2]])
                    elif gi == 1:
                        mm([(PB[b2][:, :], slabAB[:, 8 + c, cs_], binp[:, c, :], c == 0, c == 7) for c in range(8)],
                           [RAB] + R_bi, [PBR[b2]])
                    else:
                        mm([(PB[b2][:, :], slabO[:, kc, cs_], attn[:, kc, :], kc == 0, kc == KC - 1)
                            for kc in range(KC)], [RO, R_attn], [PBR[b2]])
                    dve(lambda v, b2=b2, gi=gi: v.tensor_tensor(out=mt[gi][:, :], in0=PB[b2][:, :],
                                                                in1=sgm[gi][:, :], op=ALU.mult),
                        [PBR[b2], R_sg[gi]], [R_mt[gi]])
                dve(lambda v: v.tensor_tensor(out=mt[0][:, :], in0=mt[0][:, :], in1=mt[1][:, :], op=ALU.add),
                    [R_mt[0], R_mt[1]], [R_mt[0]])
                dve(lambda v, oc=oc: v.tensor_tensor(out=mixed[:, oc, :], in0=mt[0][:, :], in1=mt[2][:, :],
                                                     op=ALU.add), [R_mt[0], R_mt[2]], [R_mx[oc]])
        wo = kview(w_out_d[l])
        for op_ in range(8):
            slab, R = wload1(wo[:, :, op_ * 256:(op_ + 1) * 256], [KC, 256])
            for j in range(2):
                oc = op_ * 2 + j
                b = nbank()
                mm([(PB[b][:, :], slab[:, kc, j * 128:(j + 1) * 128], mixed[:, kc, :], kc == 0, kc == KC - 1)
                    for kc in range(KC)], [R] + R_mx, [PBR[b]])
                dve(lambda v, b=b, oc=oc: v.scalar_tensor_tensor(
                    out=xb[0][:, oc, :], in0=PB[b][:, :], scalar=Gsc(l, 1, oc, v_), in1=xb[0][:, oc, :],
                    op0=ALU.mult, op1=ALU.add), [PBR[b]], [R_x[0][oc]])
        store_x(0, tile)
        barrier()

    ctx_sem = es.enter_context(nc.semaphore("ctx"))
    ctx_val = [0]

    def ctx_load(l, R_kc, R_vc):
        GP.deps([], [R_kc, R_vc])
        GP.wait_ev(PE.last())
        GP.wait_ev(ACT.last())
        GP.wait_ev(DVE.last())
        nc.gpsimd.dma_start(out=kctx[:, :, :], in_=kctxT_d[l]).then_inc(ctx_sem, 16)
        nc.gpsimd.dma_start(out=vctx[:, :, :], in_=vctx_d[l].rearrange("(b p) d -> p b d", p=128)).then_inc(ctx_sem, 16)
        ctx_val[0] += 32
        ev = Ev(ctx_sem, ctx_val[0])
        R_kc.set_writer(ev)
        R_vc.set_writer(ev)

    def final_norm(t, tile):
        sumsq([(xb[t][:, kc, :], R_x[t][kc]) for kc in range(KC)])
        rstd_from_bank7(D)
        for kc in range(KC):
            dve(lambda v, kc=kc: v.scalar_tensor_tensor(
                out=xb[t][:, kc, :], in0=xb[t][:, kc, :], scalar=gfin[:, kc:kc + 1], in1=rstd[:, :],
                op0=ALU.mult, op1=ALU.mult), [R_x[t][kc], R_rstd], [R_x[t][kc]])
        rx = [R_x[t][kc] for kc in range(KC)]
        if tile == 0:
            SQ.dma(yT_o[:, :, 0:512], xb[t][:, :, :], rx, [])
        elif tile == 1:
            SQ.dma(yT_o[:, :, 512:768], xb[t][:, :, 256:512], rx, [])
        elif tile == 2:
            SQ.dma(yT_o[:, :, 768:1280], xb[t][:, :, :], rx, [])
        else:
            SQ.dma(yT_o[:, :, 1280:1536], xb[t][:, :, 0:256], rx, [])

    def stop(tag):
        if STOP[0] == tag:
            raise _Stop()

    def _program():
        STS = [(0, 1), (2, 3)]
        stop("ada")
        for st in range(2):
            tiles = STS[st]
            for t in range(2):
                load_x(xT_d, t, tiles[t] * 512)
            stop("load")
            ffn(0, 0, tiles)
            stop("ffn")
            m1(0, tiles)
            stop("m1")
            for t in range(2):
                store_x(t, tiles[t])
            barrier()
        stop("ph1")
        for l in range(NL):
            for tile in range(4):
                ph2(l, tile)
                stop("ph2_%d_%d" % (l, tile))
            for st in range(2):
                tiles = STS[st]
                for t in range(2):
                    load_x(xs_s, t, tiles[t] * 512)
                ffn(l, 1, tiles)
                stop("ffn2_%d_%d" % (l, st))
                if l + 1 < NL:
                    ffn(l + 1, 0, tiles)
                    m1(l + 1, tiles)
                    for t in range(2):
                        store_x(t, tiles[t])
                else:
                    for t in range(2):
                        final_norm(t, tiles[t])
                barrier()

    try:
        _program()
    except _Stop:
        barrier()
        dbg_o = dout("dbg", [128, 2, KC, 512])
        dbh_o = dout("dbh", [128, 2, KC, 512])
        R_dbg = Res()
        for t in range(2):
            SQ.dma(dbg_o[:, t, :, :], xb[t][:, :, :], [R_dbg], [])
        for t in range(2):
            act(xb[t][:, :, :], hb[t][:, :, :], AF.Copy, [], [R_dbg])
        for E in (SP,):
            E.wait_ev(ACT.last())
        for t in range(2):
            SQ.dma(dbh_o[:, t, :, :], xb[t][:, :, :], [], [])
    for e in SQ.all_last():
        SP.wait_ev(e)
    es.close()
    return nc


def _fm(x2d):
    T, Fd = x2d.shape
    return np.ascontiguousarray(x2d.T.reshape(Fd // 128, 128, T).transpose(1, 0, 2))


def _vec_fm(v):
    sh = v.shape
    Fd = sh[-1]
    r = v.reshape(sh[:-1] + (Fd // 128, 128))
    return np.ascontiguousarray(np.moveaxis(r, -1, 0))


_CACHE = {}
_RETURN_MAPS = [False]


def kernel(x_prompt, x_sample, cache_k, cache_v, c, c_ctx, w_ada, b_ada, g_ff1, w_ff1_in, w_ff1_out, g_mix,
           w_in, attn_sink, w_attn_o, conv_a_w, conv_a_b, g_conv_a, w_a_out, conv_b_w, w_b_out, w_out, g_ff2,
           w_ff2_in, w_ff2_out, g_final):
    f32 = np.float32
    A = lambda a: np.ascontiguousarray(np.asarray(a, dtype=f32))
    x_prompt, x_sample, cache_k, cache_v, c, c_ctx = map(A, (x_prompt, x_sample, cache_k, cache_v, c, c_ctx))
    shared = {
        "w_ada": A(w_ada), "w_ff1_in": A(w_ff1_in), "w_ff1_out": A(w_ff1_out), "w_in": A(w_in),
        "w_attn_o": A(w_attn_o), "w_a_out": A(w_a_out), "w_b_out": A(w_b_out), "w_out": A(w_out),
        "w_ff2_in": A(w_ff2_in), "w_ff2_out": A(w_ff2_out),
    }
    shared["badaT"] = _vec_fm(A(b_ada))
    g3 = np.stack([A(g_ff1), A(g_mix), A(g_ff2)], axis=1)
    shared["gT"] = _vec_fm(g3)
    shared["gfin"] = _vec_fm(A(g_final))
    caw = A(conv_a_w)[:, :, 0, :]
    shared["caw"] = np.ascontiguousarray(_vec_fm(caw).transpose(0, 1, 3, 2))
    shared["cab"] = _vec_fm(A(conv_a_b))
    shared["gca"] = _vec_fm(A(g_conv_a))
    cbw = A(conv_b_w)[:, :, 0, :]
    shared["cbw"] = np.ascontiguousarray(_vec_fm(cbw).transpose(0, 1, 3, 2))
    shared["sinkb"] = np.ascontiguousarray(np.broadcast_to(A(attn_sink)[None], (128, NL, 16)))
    shared["onesm"] = np.ones((128, 128), f32)
    Rt = np.zeros((128, 128), f32)
    for base in (0, 64):
        for p in range(32):
            Rt[base + p + 32, base + p] = -1.0
            Rt[base + p, base + p + 32] = 1.0
    shared["Rt"] = Rt
    kl = np.arange(128)[:, None]
    ql = np.arange(128)[None, :]
    tri = np.stack([np.tile((kl >= ql).astype(f32), (1, 4)), np.tile((kl <= ql).astype(f32), (1, 4))], axis=1)
    shared["tri"] = np.ascontiguousarray(tri)
    inv = (np.float32(10000.0) ** (-np.arange(32, dtype=f32) / np.float32(32))).astype(f32)

    in_maps = []
    for i in range(NCORES):
        b = i // 4
        s = (i % 4) * 1024
        gp = s - 256 + np.arange(REG)
        ok = (gp >= 0) & (gp < 4096)
        xs_ = np.zeros((REG, D), f32)
        xs_[ok] = x_sample[b, gp[ok]]
        X = np.concatenate([x_prompt[2 * i], x_prompt[2 * i + 1], xs_], axis=0)
        m = dict(shared)
        m["xT"] = _fm(X)
        m["cT"] = np.ascontiguousarray(np.stack([_vec_fm(c_ctx), _vec_fm(c[b])], axis=-1))
        m["kctxT"] = np.ascontiguousarray(cache_k[b].transpose(0, 3, 2, 1))
        m["vctx"] = np.ascontiguousarray(cache_v[b].reshape(NL, 512, 512))
        gpc = np.clip(gp, 0, 4095)
        row = (gpc // 64).astype(f32)
        col = (gpc % 64).astype(f32)
        ang_r = (row[:, None] * inv[None, :]).astype(f32)
        ang_c = (col[:, None] * inv[None, :]).astype(f32)
        cr, sr = np.cos(ang_r).astype(f32), np.sin(ang_r).astype(f32)
        cc, sc_ = np.cos(ang_c).astype(f32), np.sin(ang_c).astype(f32)
        m["cosT"] = np.ascontiguousarray(np.concatenate([cr, cr, cc, cc], axis=1).T)
        m["sinT"] = np.ascontiguousarray(np.concatenate([sr, sr, sc_, sc_], axis=1).T)
        m["valid"] = np.ascontiguousarray(np.broadcast_to(ok.astype(f32)[None], (128, REG)))
        kb = np.where(ok, 0.0, NEG).astype(f32).reshape(12, 128).T
        m["keybias"] = np.ascontiguousarray(kb)
        in_maps.append(m)

    if _RETURN_MAPS[0]:
        return in_maps
    if "nc" not in _CACHE:
        _CACHE["nc"] = build_program()
    nc = _CACHE["nc"]
    res = run_bass_kernel_spmd(nc, in_maps, core_ids=list(range(NCORES)))
    y_prompt = np.zeros((16, 256, D), f32)
    y_sample = np.zeros((2, 4096, D), f32)
    nk = np.zeros((16, NL, 256, 4, 128), f32)
    nv = np.zeros((16, NL, 256, 4, 128), f32)
    for i in range(NCORES):
        r = res.results[i]
        yT = np.asarray(r["yT"])
        Y = yT.transpose(2, 1, 0).reshape(1536, D)
        y_prompt[2 * i] = Y[0:256]
        y_prompt[2 * i + 1] = Y[256:512]
        b = i // 4
        s = (i % 4) * 1024
        y_sample[b, s:s + 1024] = Y[512:1536]
        kTo = np.asarray(r["kTo"])
        vo = np.asarray(r["vo"])
        for sq_ in range(2):
            nk[2 * i + sq_] = kTo[:, :, :, sq_ * 256:(sq_ + 1) * 256].transpose(0, 3, 2, 1)
            nv[2 * i + sq_] = vo[:, sq_ * 256:(sq_ + 1) * 256, :].reshape(NL, 256, 4, 128)
    return (y_prompt, y_sample, nk, nv)
```

```python
import numpy as np
from contextlib import ExitStack
import concourse.bass as bass
import concourse.mybir as mybir
from concourse.bass_utils import run_bass_kernel_spmd

F32 = mybir.dt.float32
BF16 = mybir.dt.bfloat16
U8 = mybir.dt.uint8
AF = mybir.ActivationFunctionType
ALU = mybir.AluOpType

D = 2048
KC = 16
DFF = 5632
NL = 2
TS = 512
NTOK = 2048
REG = 1536
NCORES = 8
KOFF = 640
KL = KOFF + REG + 128
AOFF = 560
AL = AOFF + REG + 16
APOS = (16, 288)
EPS = 1e-6
NEG = -30000.0
SCALE = 128.0 ** -0.5
ARENA = 210944
NSLOT = 6
SLOT = 8192


class Ev:
    __slots__ = ("sem", "val")

    def __init__(self, sem, val):
        self.sem = sem
        self.val = val


class Res:
    __slots__ = ("w", "r", "const", "excl")

    def __init__(self, const=False, excl=False):
        self.w = None
        self.r = {}
        self.const = const
        self.excl = excl

    def add_reader(self, ev):
        if self.const:
            return
        k = id(ev.sem)
        o = self.r.get(k)
        if o is None or o.val < ev.val:
            self.r[k] = ev

    def set_writer(self, ev):
        self.w = ev
        self.r = {}


class Eng:
    def __init__(self, nc, es, name, eng):
        self.name = name
        self.eng = eng
        self.sem = es.enter_context(nc.semaphore("p_" + name))
        self.cnt = 0
        self.waited = {}

    def wait_ev(self, ev):
        if ev is None or ev.sem is self.sem:
            return
        k = id(ev.sem)
        if self.waited.get(k, 0) >= ev.val:
            return
        self.eng.wait_ge(ev.sem, ev.val)
        self.waited[k] = ev.val

    def deps(self, reads, writes):
        best = {}

        def add(ev):
            if ev is None or ev.sem is self.sem:
                return
            k = id(ev.sem)
            o = best.get(k)
            if o is None or o.val < ev.val:
                best[k] = ev

        for R in reads:
            add(R.w)
            if R.excl:
                for e in R.r.values():
                    add(e)
        for R in writes:
            add(R.w)
            for e in R.r.values():
                add(e)
        for ev in best.values():
            self.wait_ev(ev)

    def done(self, ins, reads, writes):
        self.cnt += 1
        ins.then_inc(self.sem, 1)
        ev = Ev(self.sem, self.cnt)
        for R in reads:
            R.add_reader(ev)
        for R in writes:
            R.set_writer(ev)
        return ev

    def last(self):
        return Ev(self.sem, self.cnt) if self.cnt else None


class DmaQ:
    def __init__(self, nc, es, E, nsem, name):
        self.E = E
        self.sems = [es.enter_context(nc.semaphore("%s%d" % (name, i))) for i in range(nsem)]
        self.vals = [0] * nsem
        self.i = 0

    def dma(self, out, in_, reads, writes):
        E = self.E
        E.deps(reads, writes)
        k = self.i % len(self.sems)
        self.i += 1
        if self.vals[k]:
            E.wait_ev(Ev(self.sems[k], self.vals[k]))
        self.vals[k] += 16
        E.eng.dma_start(out=out, in_=in_).then_inc(self.sems[k], 16)
        ev = Ev(self.sems[k], self.vals[k])
        for R in reads:
            R.add_reader(ev)
        for R in writes:
            R.set_writer(ev)
        return ev

    def all_last(self):
        return [Ev(s, v) for s, v in zip(self.sems, self.vals) if v]


STOP = [None]
SKIP = set()


class _Stop(Exception):
    pass


def build_program():
    nc = bass.Bass("TRN2", target_bir_lowering=False)
    es = ExitStack()

    def din(name, shape):
        if "now" in SKIP and name.startswith("w_"):
            return nc.dram_tensor(name, [1] + list(shape)[1:], F32, kind="Internal").ap()
        return nc.dram_tensor(name, list(shape), F32, kind="ExternalInput").ap()

    def dout(name, shape):
        return nc.dram_tensor(name, list(shape), F32, kind="ExternalOutput").ap()

    def dscr(name, shape, dt):
        return nc.dram_tensor(name, list(shape), dt, kind="Internal").ap()

    xT_d = din("xT", [128, KC, NTOK])
    cT_d = din("cT", [128, KC, 2])
    w_ada_d = din("w_ada", [NL, D, 9 * D])
    badaT_d = din("badaT", [128, NL, 144])
    gT_d = din("gT", [128, NL, 3, KC])
    gfin_d = din("gfin", [128, KC])
    wffin_d = [din("w_ff1_in", [NL, D, 2 * DFF]), din("w_ff2_in", [NL, D, 2 * DFF])]
    wffout_d = [din("w_ff1_out", [NL, DFF, D]), din("w_ff2_out", [NL, DFF, D])]
    w_in_d = din("w_in", [NL, D, 14336])
    w_ao_d = din("w_attn_o", [NL, D, D])
    w_aout_d = din("w_a_out", [NL, 1024, D])
    w_bout_d = din("w_b_out", [NL, 1024, D])
    w_out_d = din("w_out", [NL, D, D])
    caw_d = din("caw", [128, NL, 8, 31])
    cab_d = din("cab", [128, NL, 8])
    gca_d = din("gca", [128, NL, 8])
    cbw_d = din("cbw", [128, NL, 8, 3])
    sinkb_d = din("sinkb", [128, NL, 16])
    kctxT_d = din("kctxT", [NL, 128, 4, 512])
    vctx_d = din("vctx", [NL, 512, 512])
    cosT_d = din("cosT", [128, REG])
    sinT_d = din("sinT", [128, REG])
    valid_d = din("valid", [128, REG])
    keybias_d = din("keybias", [128, 12])
    onesm_d = din("onesm", [128, 128])
    Rt_d = din("Rt", [128, 128])
    tri_d = din("tri", [128, 2, 512])

    yT_o = dout("yT", [128, KC, 1536])
    kTo_o = dout("kTo", [NL, 128, 4, 512])
    vo_o = dout("vo", [NL, 512, 512])

    xs_s = dscr("xs", [128, KC, NTOK], F32)
    ksp_s = dscr("ksp", [128, 4, KL], BF16)
    vsp_s = dscr("vsp", [KL, 512], BF16)
    asp_s = dscr("asp", [128, 8, AL], BF16)
    cxsp_s = dscr("cxsp", [128, 8, AL], BF16)
    bgsp_s = dscr("bgsp", [128, 8, NTOK], BF16)

    def kview(w2d):
        return w2d.rearrange("(kc p) n -> p kc n", p=128)

    PE = Eng(nc, es, "pe", nc.tensor)
    ACT = Eng(nc, es, "act", nc.scalar)
    DVE = Eng(nc, es, "dve", nc.vector)
    GP = Eng(nc, es, "gp", nc.gpsimd)
    SP = Eng(nc, es, "sp", nc.sync)
    SQ = DmaQ(nc, es, SP, 8, "sq")

    arena = es.enter_context(nc.sbuf_tensor("arena", [128, ARENA], U8))
    off = [0]

    def carve_at(o, shape, dt):
        nb = 4 if dt == F32 else 2
        n = int(np.prod(shape)) * nb
        v = arena[:, o:o + n].bitcast(dt)
        if len(shape) == 2:
            v = v.rearrange("p (a b) -> p a b", a=shape[0])
        elif len(shape) == 3:
            v = v.rearrange("p (a b c) -> p a b c", a=shape[0], b=shape[1])
        return v, o + ((n + 63) // 64) * 64

    def carve(shape, dt):
        v, off[0] = carve_at(off[0], shape, dt)
        return v

    ringb = carve([NSLOT, SLOT // 2], BF16)
    modsb = carve([NL, 144, 2], F32)
    Atab = carve([NL * 3 * KC * 2], F32)
    Gtab = carve([NL * 3 * KC * 2], F32)
    badaT = carve([NL, 144], F32)
    gT = carve([NL * 3, KC], F32)
    gfin = carve([KC], F32)
    caw = carve([NL * 8, 31], F32)
    cab = carve([NL * 8], F32)
    gca = carve([NL * 8], F32)
    cbw = carve([NL * 8, 3], F32)
    sinkb = carve([NL * 16], F32)
    sexp = carve([NL * 16], F32)
    cTf = carve([KC, 2], F32)
    csil = carve([KC, 2], BF16)
    ones = carve([128], BF16)
    Rt = carve([128], BF16)
    tri = carve([2, 512], BF16)
    keybias = carve([12], F32)
    valid = carve([REG], BF16)
    cosb = [carve([512], F32) for _ in range(2)]
    sinb = [carve([512], F32) for _ in range(2)]
    sqb = [carve([512], BF16) for _ in range(3)]
    rstd = carve([512], F32)
    ntmp = [carve([512], F32) for _ in range(3)]
    xb = [carve([KC, 512], F32), None]
    hb = [carve([KC, 512], BF16), None]
    PH0 = off[0]
    xb[1] = carve([KC, 512], F32)
    hb[1] = carve([KC, 512], BF16)
    PHT = off[0]
    assert ARENA - PHT >= 24576, (ARENA - PHT)
    o = PHT
    hid = []
    for i in range(2):
        v, o = carve_at(o, [2, 1024], BF16)
        hid.append(v)
    su = []
    for i in range(3):
        v, o = carve_at(o, [512], BF16)
        su.append(v)
    stb = []
    for i in range(4):
        v, o = carve_at(o, [512], BF16)
        stb.append(v)
    stf = []
    for i in range(2):
        v, o = carve_at(o, [512], F32)
        stf.append(v)
    rxb1, o = carve_at(o, [512], BF16)
    rt1a, o = carve_at(o, [512], F32)
    rt2a, o = carve_at(o, [512], F32)
    assert o <= ARENA, o
    o = PH0
    attn, o = carve_at(o, [KC, 512], BF16)
    afeat, o = carve_at(o, [8, 512], BF16)
    binp, o = carve_at(o, [8, 512], BF16)
    T0 = o
    awin, o = carve_at(T0, [8, 576], BF16)
    acv, o = carve_at(o, [8, 512], F32)
    assert o <= ARENA
    cxwin, o = carve_at(T0, [8, 544], BF16)
    bgwin, o = carve_at(o, [8, 512], BF16)
    cacc, o = carve_at(o, [512], F32)
    assert o <= ARENA
    qg, o = carve_at(T0, [4, 512], BF16)
    kwin, o = carve_at(o, [4, 768], BF16)
    vwin, o = carve_at(o, [6, 512], BF16)
    kctx, o = carve_at(o, [4, 512], BF16)
    vctx, o = carve_at(o, [4, 512], BF16)
    pT = []
    for i in range(3):
        v, o = carve_at(o, [512], BF16)
        pT.append(v)
    dtmp, o = carve_at(o, [512], F32)
    rxb2 = []
    rt1b = []
    rt2b = []
    for i in range(2):
        v, o = carve_at(o, [512], BF16)
        rxb2.append(v)
        v, o = carve_at(o, [512], F32)
        rt1b.append(v)
        v, o = carve_at(o, [512], F32)
        rt2b.append(v)
    assert o <= ARENA, o
    mixed, o = carve_at(T0, [KC, 512], BF16)
    sgm = []
    for i in range(3):
        v, o = carve_at(o, [512], BF16)
        sgm.append(v)
    mt = []
    for i in range(3):
        v, o = carve_at(o, [512], F32)
        mt.append(v)
    assert o <= ARENA, o

    PB = [es.enter_context(nc.psum_tensor("pb%d" % i, [128, 512], F32)) for i in range(8)]
    PBR = [Res(excl=True) for _ in range(8)]
    rot = [0]

    pool = [5]

    def nbank():
        i = rot[0] % pool[0]
        rot[0] += 1
        return i

    def act(out, in_, func, reads, writes, bias=None, scale=None):
        ACT.deps(reads, writes)
        kw = {}
        if bias is not None:
            kw["bias"] = bias
        if scale is not None:
            kw["scale"] = scale
        ins = nc.scalar.activation(out=out, in_=in_, func=func, **kw)
        return ACT.done(ins, reads, writes)

    def dve(fn, reads, writes):
        DVE.deps(reads, writes)
        ins = fn(nc.vector)
        return DVE.done(ins, reads, writes)

    def mm(specs, reads, writes):
        PE.deps(reads, writes)
        ins = None
        for (o_, l_, r_, st_, sp_) in specs:
            ins = nc.tensor.matmul(o_, lhsT=l_, rhs=r_, start=st_, stop=sp_)
        return PE.done(ins, reads, writes)

    def barrier():
        evs = [PE.last(), ACT.last(), DVE.last()] + SQ.all_last()
        for E in (PE, ACT, DVE, SP):
            for e in evs:
                E.wait_ev(e)

    ring_sems = [es.enter_context(nc.semaphore("ring%d" % i)) for i in range(NSLOT)]
    ring_vals = [0] * NSLOT
    ring_res = [Res() for _ in range(NSLOT)]
    ring_n = [0]

    def wload(parts, shape):
        s = ring_n[0] % NSLOT
        ring_n[0] += 1
        R = ring_res[s]
        GP.deps([], [R])
        n = int(np.prod(shape))
        v = ringb[:, s, 0:n]
        if len(shape) == 2:
            v = v.rearrange("p (a b) -> p a b", a=shape[0])
        for dstf, src in parts:
            nc.gpsimd.dma_start(out=dstf(v), in_=src).then_inc(ring_sems[s], 16)
            ring_vals[s] += 16
        R.set_writer(Ev(ring_sems[s], ring_vals[s]))
        return v, R

    def wload1(src, shape):
        return wload([(lambda v: v, src)], shape)

    def gload(dst, src, R):
        raise NotImplementedError

    setup_sem = [es.enter_context(nc.semaphore("setup_hw")), es.enter_context(nc.semaphore("setup_sw"))]
    setup_n = [0, 0]

    def sload(dst, src, cast=False):
        eng = nc.gpsimd if cast else nc.sync
        i = 1 if cast else 0
        eng.dma_start(out=dst, in_=src).then_inc(setup_sem[i], 16)
        setup_n[i] += 16

    sload(badaT, badaT_d)
    sload(gT, gT_d.rearrange("p l s k -> p (l s) k"))
    sload(gfin, gfin_d)
    sload(caw, caw_d.rearrange("p l c j -> p (l c) j"))
    sload(cab, cab_d.rearrange("p l c -> p (l c)"))
    sload(gca, gca_d.rearrange("p l c -> p (l c)"))
    sload(cbw, cbw_d.rearrange("p l c j -> p (l c) j"))
    sload(sinkb, sinkb_d.rearrange("p l h -> p (l h)"))
    sload(cTf, cT_d)
    sload(keybias, keybias_d)
    sload(ones, onesm_d, cast=True)
    sload(Rt, Rt_d, cast=True)
    sload(tri, tri_d, cast=True)
    sload(valid, valid_d, cast=True)
    setup_evs = [Ev(setup_sem[0], setup_n[0]), Ev(setup_sem[1], setup_n[1])]
    R_zero = Res()
    dve(lambda v: v.memset(stb[0][:, 0:128], 0.0), [], [R_zero])
    R_asp = Res()
    R_cxsp = Res()
    zsrc = stb[0][:, 0:128].rearrange("p (c j) -> p c j", c=8)
    for p0 in (0, 272, 544):
        SQ.dma(asp_s[:, :, p0:p0 + 16], zsrc, [R_zero], [])
        SQ.dma(cxsp_s[:, :, p0:p0 + 16], zsrc, [R_zero], [])
    for e in SQ.all_last():
        R_asp.add_reader(e)
    R_attn = Res()
    dve(lambda v: v.memset(attn[:, :, :], 0.0), [], [R_attn])
    for E in (PE, ACT, DVE, SP):
        for e_ in setup_evs:
            E.wait_ev(e_)
    barrier()

    RC = Res(const=True)

    R_csil = Res()
    act(csil[:, :, :], cTf[:, :, :], AF.Silu, [RC], [R_csil])
    R_mods = Res()
    for l in range(1):
        wv = kview(w_ada_d[l if 'now' not in SKIP else 0])
        for s2 in range(72 if "ada" not in SKIP else 1):
            slab, R = wload1(wv[:, :, s2 * 256:(s2 + 1) * 256], [KC, 256])
            specs = []
            for j in range(2):
                n = s2 * 2 + j
                for kc in range(KC):
                    specs.append((PB[7][:, 2 * n:2 * n + 2], slab[:, kc, j * 128:(j + 1) * 128],
                                  csil[:, kc, :], kc == 0, kc == KC - 1))
            mm(specs, [R, R_csil], [PBR[7]])
        pv = PB[7][:, 0:288].rearrange("p (n v) -> p n v", v=2)
        for v_ in range(2):
            dve(lambda v, v_=v_: v.tensor_tensor(out=modsb[:, l, :, v_], in0=pv[:, :, v_], in1=badaT[:, l, :],
                                                 op=ALU.add), [PBR[7]], [R_mods])
    A3 = Atab.rearrange("p (q k v) -> p q k v", k=KC, v=2)
    G3 = Gtab.rearrange("p (q k v) -> p q k v", k=KC, v=2)

    def derive_tables(l):
        for s_ in range(3):
            for v_ in range(2):
                dve(lambda v, l=l, s_=s_, v_=v_: v.scalar_tensor_tensor(
                    out=A3[:, l * 3 + s_, :, v_], in0=modsb[:, l, (3 * s_ + 1) * 16:(3 * s_ + 2) * 16, v_],
                    scalar=1.0, in1=gT[:, l * 3 + s_, :], op0=ALU.add, op1=ALU.mult), [R_mods], [R_mods])
                dve(lambda v, l=l, s_=s_, v_=v_: v.tensor_scalar(
                    out=G3[:, l * 3 + s_, :, v_], in0=modsb[:, l, (3 * s_ + 2) * 16:(3 * s_ + 3) * 16, v_],
                    scalar1=(1.0 if s_ == 1 else 0.5), scalar2=None, op0=ALU.mult), [R_mods], [R_mods])

    ada1_next = [0]

    def ada1_slabs(n):
        items = []
        wv1 = kview(w_ada_d[1 if 'now' not in SKIP else 0])
        for i in range(n):
            s2 = ada1_next[0]
            if s2 >= 72:
                break
            ada1_next[0] += 1
            slab, R = wload1(wv1[:, :, s2 * 256:(s2 + 1) * 256], [KC, 256])
            specs = []
            for j in range(2):
                for kc in range(KC):
                    specs.append((PB[6][:, 4 * i + 2 * j:4 * i + 2 * j + 2], slab[:, kc, j * 128:(j + 1) * 128],
                                  csil[:, kc, :], kc == 0, kc == KC - 1))
            mm(specs, [R, R_csil], [PBR[6]])
            items.append((s2, 4 * i))
        return items

    def ada1_evac(items):
        for (s2, c0) in items:
            pv1 = PB[6][:, c0:c0 + 4].rearrange("p (n v) -> p n v", v=2)
            for v_ in range(2):
                dve(lambda v, v_=v_, s2=s2, pv1=pv1: v.tensor_tensor(
                    out=modsb[:, 1, 2 * s2:2 * s2 + 2, v_], in0=pv1[:, :, v_], in1=badaT[:, 1, 2 * s2:2 * s2 + 2],
                    op=ALU.add), [PBR[6]], [R_mods])
        if ada1_next[0] >= 72 and items:
            derive_tables(1)

    for l in range(1):
        for s_ in range(3):
            for v_ in range(2):
                dve(lambda v, l=l, s_=s_, v_=v_: v.scalar_tensor_tensor(
                    out=A3[:, l * 3 + s_, :, v_], in0=modsb[:, l, (3 * s_ + 1) * 16:(3 * s_ + 2) * 16, v_],
                    scalar=1.0, in1=gT[:, l * 3 + s_, :], op0=ALU.add, op1=ALU.mult), [R_mods], [R_mods])
                dve(lambda v, l=l, s_=s_, v_=v_: v.tensor_scalar(
                    out=G3[:, l * 3 + s_, :, v_], in0=modsb[:, l, (3 * s_ + 2) * 16:(3 * s_ + 3) * 16, v_],
                    scalar1=(1.0 if s_ == 1 else 0.5), scalar2=None, op0=ALU.mult), [R_mods], [R_mods])
    act(sexp[:, :], sinkb[:, :], AF.Exp, [RC], [R_mods])
    for E in (PE, ACT, DVE, SP):
        E.wait_ev(R_mods.w)
    def Asc(l, s_, kc, v_):
        i = ((l * 3 + s_) * KC + kc) * 2 + v_
        return Atab[:, i:i + 1]

    def Gsc(l, s_, kc, v_):
        i = ((l * 3 + s_) * KC + kc) * 2 + v_
        return Gtab[:, i:i + 1]

    def Bsc(l, s_, kc, v_):
        return modsb[:, l, 3 * s_ * 16 + kc, v_:v_ + 1]

    R_x = [[Res() for _ in range(KC)] for _ in range(2)]
    R_h = [[Res() for _ in range(KC)] for _ in range(2)]
    R_sq = [Res() for _ in range(3)]
    R_rstd = Res()
    R_nt = [Res() for _ in range(3)]
    R_hid = [[[Res() for _ in range(2)] for _ in range(2)] for _ in range(2)]
    R_su = [Res() for _ in range(3)]
    R_stb = [Res() for _ in range(4)]
    R_stf = [Res() for _ in range(2)]
    R_cs = [Res() for _ in range(2)]
    R_xs = [Res() for _ in range(4)]
    R_ksp = [Res() for _ in range(4)]
    R_vsp = [Res() for _ in range(4)]
    R_aspt = [Res() for _ in range(4)]
    R_cxt = [Res() for _ in range(4)]
    R_bgt = [Res() for _ in range(4)]
    R_out = Res()
    cnt = {"sq": 0, "nt": 0, "su": 0, "stb": 0, "stf": 0}

    def rr(name, n):
        i = cnt[name] % n
        cnt[name] += 1
        return i

    def tile_vec(tile):
        return 0 if tile == 0 else 1

    def rstd_from_bank7(nparts):
        dve(lambda v: v.tensor_scalar(out=rstd[:, :], in0=PB[7][:, :], scalar1=1.0 / nparts, scalar2=EPS,
                                      op0=ALU.mult, op1=ALU.add), [PBR[7]], [R_rstd])
        act(rstd[:, :], rstd[:, :], AF.Sqrt, [R_rstd], [R_rstd])
        dve(lambda v: v.reciprocal(out=rstd[:, :], in_=rstd[:, :]), [R_rstd], [R_rstd])

    def sumsq(srcs):
        n = len(srcs)
        for i, (ap, R) in enumerate(srcs):
            b = rr("sq", 3)
            act(sqb[b][:, :], ap, AF.Square, [R], [R_sq[b]])
            mm([(PB[7][:, :], ones[:, :], sqb[b][:, :], i == 0, i == n - 1)], [R_sq[b]], [PBR[7]])

    def norm_mod(l, s_, t, v_):
        sumsq([(xb[t][:, kc, :], R_x[t][kc]) for kc in range(KC)])
        rstd_from_bank7(D)
        for kc in range(KC):
            b = rr("nt", 3)
            dve(lambda v, kc=kc, b=b: v.scalar_tensor_tensor(
                out=ntmp[b][:, :], in0=xb[t][:, kc, :], scalar=Asc(l, s_, kc, v_), in1=rstd[:, :],
                op0=ALU.mult, op1=ALU.mult), [R_x[t][kc], R_rstd], [R_nt[b]])
            act(hb[t][:, kc, :], ntmp[b][:, :], AF.Identity, [R_nt[b]], [R_h[t][kc]],
                bias=Bsc(l, s_, kc, v_), scale=1.0)

    def ffn(l, which, tiles):
        if "ffn" in SKIP:
            return
        s_ = 0 if which == 0 else 2
        nt_ = len(tiles)
        for t in range(nt_):
            norm_mod(l, s_, t, tile_vec(tiles[t]))
        win = kview(wffin_d[which][l])
        wout = kview(wffout_d[which][l])
        allh = [R_h[t][kc] for t in range(nt_) for kc in range(KC)]

        def in_units(g):
            hbuf = g % 2
            slabU, RU = wload1(win[:, :, g * 256:(g + 1) * 256], [KC, 256])
            slabV, RV = wload1(win[:, :, DFF + g * 256:DFF + (g + 1) * 256], [KC, 256])
            units = []
            for j in range(2):
                for t in range(nt_):
                    units.append(lambda j=j, t=t: in_unit(hbuf, slabU, RU, slabV, RV, j, t))
            return units

        def in_unit(hbuf, slabU, RU, slabV, RV, j, t):
            if True:
                if True:
                    bu = nbank()
                    mm([(PB[bu][:, :], slabU[:, kc, j * 128:(j + 1) * 128], hb[t][:, kc, :], kc == 0, kc == KC - 1)
                        for kc in range(KC)], [RU] + [R_h[t][kc] for kc in range(KC)], [PBR[bu]])
                    si = rr("su", 3)
                    act(su[si][:, :], PB[bu][:, :], AF.Silu, [PBR[bu]], [R_su[si]])
                    bv = nbank()
                    mm([(PB[bv][:, :], slabV[:, kc, j * 128:(j + 1) * 128], hb[t][:, kc, :], kc == 0, kc == KC - 1)
                        for kc in range(KC)], [RV] + [R_h[t][kc] for kc in range(KC)], [PBR[bv]])
                    dve(lambda v, bv=bv, si=si, j=j, t=t: v.tensor_tensor(
                        out=hid[hbuf][:, j, t * 512:(t + 1) * 512], in0=PB[bv][:, :], in1=su[si][:, :],
                        op=ALU.mult), [PBR[bv], R_su[si]], [R_hid[hbuf][j][t]])

        def out_units(g):
            hbuf = g % 2
            slabO, RO = wload1(wout[:, 2 * g:2 * g + 2, :], [2, 2048])
            return [lambda q=q: out_unit(hbuf, slabO, RO, range(4 * q, 4 * q + 4)) for q in range(4)]

        def out_unit(hbuf, slabO, RO, ocs):
            for oc in ocs:
                for t in range(nt_):
                    b = nbank()
                    mm([(PB[b][:, :], slabO[:, j, oc * 128:(oc + 1) * 128], hid[hbuf][:, j, t * 512:(t + 1) * 512],
                         j == 0, j == 1) for j in range(2)],
                       [RO, R_hid[hbuf][0][t], R_hid[hbuf][1][t]], [PBR[b]])
                    dve(lambda v, b=b, oc=oc, t=t: v.scalar_tensor_tensor(
                        out=xb[t][:, oc, :], in0=PB[b][:, :], scalar=Gsc(l, s_, oc, tile_vec(tiles[t])),
                        in1=xb[t][:, oc, :], op0=ALU.mult, op1=ALU.add), [PBR[b]], [R_x[t][oc]])

        NG = DFF // 256
        pool[0] = 7
        for u in in_units(0):
            u()
        for g in range(NG):
            outs = out_units(g)
            ins = in_units(g + 1) if g + 1 < NG else []
            for i in range(4):
                if ins:
                    ins[i]()
                outs[i]()
        pool[0] = 5

    def rope(bank, cs, out_ap, R_outs, xbuf_, t1_, t2_, Rtmp):
        Rxb, Rt1, Rt2 = Rtmp
        if "altdst" in SKIP:
            act(stb[3][:, :], PB[bank][:, :], AF.Copy, [PBR[bank]], [Rxb])
        elif "noact" not in SKIP:
            act(xbuf_[:, :], PB[bank][:, :], AF.Copy, [PBR[bank]], [Rxb])
        if "norot" in SKIP:
            b2 = bank
        else:
            b2 = nbank()
            mm([(PB[b2][:, :], Rt[:, :], xbuf_[:, :], True, True)], [Rxb], [PBR[b2]])
        dve(lambda v: v.tensor_tensor(out=t1_[:, :], in0=PB[bank][:, :], in1=cosb[cs][:, :], op=ALU.mult),
            [PBR[bank], R_cs[cs]] + ([Rxb] if "seq" in SKIP else []), [Rt1])
        dve(lambda v: v.tensor_tensor(out=t2_[:, :], in0=PB[b2][:, :], in1=sinb[cs][:, :], op=ALU.mult),
            [PBR[b2], R_cs[cs]], [Rt2])
        dve(lambda v: v.tensor_tensor(out=out_ap, in0=t1_[:, :], in1=t2_[:, :], op=ALU.add),
            [Rt1, Rt2], R_outs)

    def load_cs(cs, tile):
        rp0 = (tile - 1) * 512
        SQ.dma(cosb[cs][:, :], cosT_d[:, rp0:rp0 + 512], [], [R_cs[cs]])
        SQ.dma(sinb[cs][:, :], sinT_d[:, rp0:rp0 + 512], [], [R_cs[cs]])

    R_rm1 = (Res(), Res(), Res())

    def m1(l, tiles):
        nt_ = len(tiles)
        wv = kview(w_in_d[l])
        for t in range(nt_):
            norm_mod(l, 1, t, tile_vec(tiles[t]))
            if tiles[t] > 0:
                load_cs(t, tiles[t])
        hall = [[R_h[t][kc] for kc in range(KC)] for t in range(nt_)]

        def proj(slab, R, j, t):
            b = nbank()
            mm([(PB[b][:, :], slab[:, kc, j * 128:(j + 1) * 128], hb[t][:, kc, :], kc == 0, kc == KC - 1)
                for kc in range(KC)], [R] + hall[t], [PBR[b]])
            return b

        for hp in range(2):
            slab, R = wload1(wv[:, :, 2048 + hp * 256:2048 + (hp + 1) * 256], [KC, 256])
            for j in range(2):
                head = 2 * hp + j
                for t in range(nt_):
                    tile = tiles[t]
                    b = proj(slab, R, j, t)
                    si = rr("stb", 4)
                    if tile == 0:
                        act(stb[si][:, :], PB[b][:, :], AF.Copy, [PBR[b]], [R_stb[si]])
                        fi = rr("stf", 2)
                        dve(lambda v, b=b, fi=fi: v.tensor_copy(out=stf[fi][:, :], in_=PB[b][:, :]),
                            [PBR[b]], [R_stf[fi]])
                        SQ.dma(kTo_o[l, :, head, :], stf[fi][:, :], [R_stf[fi]], [])
                        SQ.dma(ksp_s[:, head, 0:512], stb[si][:, :], [R_stb[si]], [])
                        stop('m1k0')
                    else:
                        rope(b, t, stb[si][:, :], [R_stb[si]], rxb1, rt1a, rt2a, R_rm1)
                        p0 = KOFF + (tile - 1) * 512
                        SQ.dma(ksp_s[:, head, p0:p0 + 512], stb[si][:, :], [R_stb[si]], [])
                        stop('m1k1')
        stop('m1k')
        for half in range(2):
            slab, R = wload1(wv[:, :, 2560 + half * 256:2560 + (half + 1) * 256], [KC, 256])
            for t in range(nt_):
                tile = tiles[t]
                for blk in range(4):
                    b = nbank()
                    mm([(PB[b][:, 0:256], hb[t][:, kc, blk * 128:(blk + 1) * 128], slab[:, kc, :], kc == 0,
                         kc == KC - 1) for kc in range(KC)], [R] + hall[t], [PBR[b]])
                    si = rr("stb", 4)
                    act(stb[si][:, 0:256], PB[b][:, 0:256], AF.Copy, [PBR[b]], [R_stb[si]])
                    if tile == 0:
                        r0 = blk * 128
                        fi = rr("stf", 2)
                        dve(lambda v, b=b, fi=fi: v.tensor_copy(out=stf[fi][:, 0:256], in_=PB[b][:, 0:256]),
                            [PBR[b]], [R_stf[fi]])
                        SQ.dma(vo_o[l, r0:r0 + 128, half * 256:(half + 1) * 256], stf[fi][:, 0:256],
                               [R_stf[fi]], [])
                    else:
                        r0 = KOFF + (tile - 1) * 512 + blk * 128
                    SQ.dma(vsp_s[r0:r0 + 128, half * 256:(half + 1) * 256], stb[si][:, 0:256], [R_stb[si]],
                           [])

        stop('m1v')

        def conv_store(scr, Rscr, c, tile, si):
            if tile == 0:
                for sg_ in range(2):
                    p0 = APOS[sg_]
                    SQ.dma(scr[:, c, p0:p0 + 256], stb[si][:, sg_ * 256:(sg_ + 1) * 256], [R_stb[si]], [])
            else:
                p0 = AOFF + (tile - 1) * 512
                SQ.dma(scr[:, c, p0:p0 + 512], stb[si][:, :], [R_stb[si]], [])

        for cp in range(4):
            slabA, RA = wload1(wv[:, :, 3072 + cp * 256:3072 + (cp + 1) * 256], [KC, 256])
            slabG, RG = wload1(wv[:, :, 4096 + cp * 256:4096 + (cp + 1) * 256], [KC, 256])
            for j in range(2):
                c = 2 * cp + j
                for t in range(nt_):
                    tile = tiles[t]
                    bg_ = proj(slabG, RG, j, t)
                    ui = rr("su", 3)
                    act(su[ui][:, :], PB[bg_][:, :], AF.Sigmoid, [PBR[bg_]], [R_su[ui]])
                    if tile > 0:
                        rp0 = (tile - 1) * 512
                        dve(lambda v, ui=ui, rp0=rp0: v.tensor_tensor(out=su[ui][:, :], in0=su[ui][:, :],
                                                                      in1=valid[:, rp0:rp0 + 512], op=ALU.mult),
                            [R_su[ui]], [R_su[ui]])
                    ba = proj(slabA, RA, j, t)
                    si = rr("stb", 4)
                    dve(lambda v, ba=ba, ui=ui, si=si: v.tensor_tensor(out=stb[si][:, :], in0=PB[ba][:, :],
                                                                       in1=su[ui][:, :], op=ALU.mult),
                        [PBR[ba], R_su[ui]], [R_stb[si]])
                    conv_store(asp_s, R_aspt, c, tile, si)
        stop('m1a')
        for cp in range(4):
            slabB, RB = wload1(wv[:, :, 5120 + cp * 256:5120 + (cp + 1) * 256], [KC, 256])
            slabC, RCg = wload1(wv[:, :, 6144 + cp * 256:6144 + (cp + 1) * 256], [KC, 256])
            slabX, RX = wload1(wv[:, :, 7168 + cp * 256:7168 + (cp + 1) * 256], [KC, 256])
            for j in range(2):
                c = 2 * cp + j
                for t in range(nt_):
                    tile = tiles[t]
                    bb = proj(slabB, RB, j, t)
                    si = rr("stb", 4)
                    act(stb[si][:, :], PB[bb][:, :], AF.Copy, [PBR[bb]], [R_stb[si]])
                    SQ.dma(bgsp_s[:, c, tile * 512:(tile + 1) * 512], stb[si][:, :], [R_stb[si]], [])
                    bx = proj(slabX, RX, j, t)
                    ui = rr("su", 3)
                    act(su[ui][:, :], PB[bx][:, :], AF.Copy, [PBR[bx]], [R_su[ui]])
                    if tile > 0:
                        rp0 = (tile - 1) * 512
                        dve(lambda v, ui=ui, rp0=rp0: v.tensor_tensor(out=su[ui][:, :], in0=su[ui][:, :],
                                                                      in1=valid[:, rp0:rp0 + 512], op=ALU.mult),
                            [R_su[ui]], [R_su[ui]])
                    bc = proj(slabC, RCg, j, t)
                    si = rr("stb", 4)
                    dve(lambda v, bc=bc, ui=ui, si=si: v.tensor_tensor(out=stb[si][:, :], in0=PB[bc][:, :],
                                                                       in1=su[ui][:, :], op=ALU.mult),
                        [PBR[bc], R_su[ui]], [R_stb[si]])
                    conv_store(cxsp_s, R_cxt, c, tile, si)

    def load_x(src, t, c0):
        SQ.dma(xb[t][:, :, :], src[:, :, c0:c0 + 512], [],
               [R_x[t][kc] for kc in range(KC)])

    def store_x(t, tile):
        SQ.dma(xs_s[:, :, tile * 512:(tile + 1) * 512], xb[t][:, :, :], [R_x[t][kc] for kc in range(KC)],
               [])

    R_T = Res()

    def ph2(l, tile):
        sample = tile > 0
        v_ = tile_vec(tile)
        wv = kview(w_in_d[l])
        barrier()
        load_x(xs_s, 0, tile * 512)
        if sample:
            load_cs(0, tile)
        norm_mod(l, 1, 0, v_)
        hall = [R_h[0][kc] for kc in range(KC)]
        rp0 = (tile - 1) * 512
        if sample:
            segs = [(0, 512, 0)]
        else:
            segs = [(0, 256, 0), (256, 256, 288)]
        R_aw = Res()
        if sample:
            SQ.dma(awin[:, :, 0:542], asp_s[:, :, AOFF + rp0 - 15:AOFF + rp0 + 527],
                   [], [R_aw])
        else:
            for sg_ in range(2):
                p0 = APOS[sg_] - 15
                SQ.dma(awin[:, :, sg_ * 288:sg_ * 288 + 286], asp_s[:, :, p0:p0 + 286], [], [R_aw])
        ada_items = ada1_slabs(9) if l == 0 else []
        R_acv = [Res() for _ in range(8)]
        for c in range(8):
            for (o0, ln, w0) in segs:
                for j in range(31):
                    if j == 0:
                        dve(lambda v, c=c, o0=o0, ln=ln, w0=w0: v.tensor_scalar(
                            out=acv[:, c, o0:o0 + ln], in0=awin[:, c, w0:w0 + ln], scalar1=caw[:, l * 8 + c, 0:1],
                            scalar2=cab[:, l * 8 + c:l * 8 + c + 1], op0=ALU.mult, op1=ALU.add), [R_aw], [R_acv[c]])
                    else:
                        dve(lambda v, c=c, o0=o0, ln=ln, w0=w0, j=j: v.scalar_tensor_tensor(
                            out=acv[:, c, o0:o0 + ln], in0=awin[:, c, w0 + j:w0 + j + ln],
                            scalar=caw[:, l * 8 + c, j:j + 1], in1=acv[:, c, o0:o0 + ln], op0=ALU.mult,
                            op1=ALU.add), [R_aw], [R_acv[c]])
        sumsq([(acv[:, c, :], R_acv[c]) for c in range(8)])
        rstd_from_bank7(1024)
        R_af = [Res() for _ in range(8)]
        for c in range(8):
            b = rr("nt", 3)
            dve(lambda v, c=c, b=b: v.scalar_tensor_tensor(
                out=ntmp[b][:, :], in0=acv[:, c, :], scalar=gca[:, l * 8 + c:l * 8 + c + 1], in1=rstd[:, :],
                op0=ALU.mult, op1=ALU.mult), [R_acv[c], R_rstd], [R_nt[b]])
            act(afeat[:, c, :], ntmp[b][:, :], AF.Silu, [R_nt[b]], [R_af[c]])
        ada1_evac(ada_items)
        barrier()
        R_cw = Res()
        R_bw = Res()
        if sample:
            SQ.dma(cxwin[:, :, 0:514], cxsp_s[:, :, AOFF + rp0 - 1:AOFF + rp0 + 513],
                   [], [R_cw])
            bsegs = [(0, 512, 0)]
        else:
            for sg_ in range(2):
                p0 = APOS[sg_] - 1
                SQ.dma(cxwin[:, :, sg_ * 272:sg_ * 272 + 258], cxsp_s[:, :, p0:p0 + 258], [], [R_cw])
            bsegs = [(0, 256, 0), (256, 256, 272)]
        SQ.dma(bgwin[:, :, :], bgsp_s[:, :, tile * 512:(tile + 1) * 512], [], [R_bw])
        ada_items = ada1_slabs(9) if l == 0 else []
        R_ca = Res()
        R_bi = [Res() for _ in range(8)]
        for c in range(8):
            for (o0, ln, w0) in bsegs:
                dve(lambda v, c=c, o0=o0, ln=ln, w0=w0: v.tensor_scalar(
                    out=cacc[:, o0:o0 + ln], in0=cxwin[:, c, w0:w0 + ln], scalar1=cbw[:, l * 8 + c, 0:1],
                    scalar2=None, op0=ALU.mult), [R_cw], [R_ca])
                for j in (1, 2):
                    dve(lambda v, c=c, o0=o0, ln=ln, w0=w0, j=j: v.scalar_tensor_tensor(
                        out=cacc[:, o0:o0 + ln], in0=cxwin[:, c, w0 + j:w0 + j + ln],
                        scalar=cbw[:, l * 8 + c, j:j + 1], in1=cacc[:, o0:o0 + ln], op0=ALU.mult, op1=ALU.add),
                        [R_cw], [R_ca])
            dve(lambda v, c=c: v.tensor_tensor(out=binp[:, c, :], in0=cacc[:, :], in1=bgwin[:, c, :],
                                               op=ALU.mult), [R_ca, R_bw], [R_bi[c]])
        ada1_evac(ada_items)
        barrier()
        R_kw = Res()
        R_vw = Res()
        R_kc = Res()
        R_vc = Res()
        if sample:
            p0 = KOFF + rp0 - 128
            SQ.dma(kwin[:, :, :], ksp_s[:, :, p0:p0 + 768], [], [R_kw])
            SQ.dma(vwin[:, :, :], vsp_s[p0:p0 + 768, :].rearrange("(b p) d -> p b d", p=128),
                   [], [R_vw])
        else:
            SQ.dma(kwin[:, :, 0:512], ksp_s[:, :, 0:512], [], [R_kw])
            SQ.dma(vwin[:, 0:4, :], vsp_s[0:512, :].rearrange("(b p) d -> p b d", p=128), [], [R_vw])
        if sample:
            ctx_load(l, R_kc, R_vc)
        R_q = [Res() for _ in range(4)]
        R_pT = [Res() for _ in range(3)]
        R_dt = Res()
        R_rb = [(Res(), Res(), Res()) for _ in range(2)]
        pcount = 0
        if sample:
            blk0 = (tile - 1) * 4
            lo, hi = (1, 10) if l == 0 else (2, 9)
            qbs = [qb for qb in range(4) if lo <= blk0 + qb <= hi]
        else:
            qbs = [0, 1, 2, 3]
        for g in range(4):
            slabs = []
            for hh in range(2):
                slabs.append(wload1(wv[:, :, g * 512 + hh * 256:g * 512 + (hh + 1) * 256], [KC, 256]))
            for j in range(4):
                slab, R = slabs[j // 2]
                b = nbank()
                mm([(PB[b][:, :], slab[:, kc, (j % 2) * 128:(j % 2 + 1) * 128], hb[0][:, kc, :], kc == 0,
                     kc == KC - 1) for kc in range(KC)], [R] + hall, [PBR[b]])
                if sample:
                    ri = j % 2
                    rope(b, 0, qg[:, j, :], [R_q[j]], rxb2[ri], rt1b[ri], rt2b[ri], R_rb[ri])
                else:
                    act(qg[:, j, :], PB[b][:, :], AF.Copy, [PBR[b]], [R_q[j]])
            for qb in qbs:
                chunks = []
                if sample:
                    for w_ in range(3):
                        kb = blk0 + qb - 1 + w_
                        chunks.append((kwin[:, g, (qb + w_) * 128:(qb + w_ + 1) * 128],
                                       vwin[:, qb + w_, g * 128:(g + 1) * 128],
                                       keybias[:, kb:kb + 1] if 0 <= kb < 12 else None,
                                       (0 if w_ == 0 else (1 if w_ == 2 else None)), [R_kw, R_vw]))
                    for c_ in range(4):
                        chunks.append((kctx[:, g, c_ * 128:(c_ + 1) * 128], vctx[:, c_, g * 128:(g + 1) * 128],
                                       None, None, [R_kc, R_vc]))
                else:
                    sq_ = qb // 2
                    for c_ in range(2):
                        kk = sq_ * 2 + c_
                        chunks.append((kwin[:, g, kk * 128:(kk + 1) * 128], vwin[:, kk, g * 128:(g + 1) * 128],
                                       None, None, [R_kw, R_vw]))
                nch = len(chunks)
                rq = qg[:, :, qb * 128:(qb + 1) * 128]
                for ci, (kap, vap, bias, msk, rds) in enumerate(chunks):
                    b = nbank()
                    mm([(PB[b][:, :].rearrange("p (h q) -> p h q", h=4), kap, rq, True, True)], rds + R_q, [PBR[b]])
                    pi = pcount % 3
                    pcount += 1
                    act(pT[pi][:, :], PB[b][:, :], AF.Exp, [PBR[b]], [R_pT[pi]], bias=bias, scale=SCALE)
                    if msk is not None:
                        dve(lambda v, pi=pi, msk=msk: v.tensor_tensor(out=pT[pi][:, :], in0=pT[pi][:, :],
                                                                      in1=tri[:, msk, :], op=ALU.mult),
                            [R_pT[pi]], [R_pT[pi]])
                    mm([(PB[5][:, :], vap, pT[pi][:, :], ci == 0, ci == nch - 1)], rds + [R_pT[pi]], [PBR[5]])
                    mm([(PB[6][:, :], ones[:, :], pT[pi][:, :], ci == 0, ci == nch - 1)], [R_pT[pi]], [PBR[6]])
                for j in range(4):
                    h_ = l * 16 + 4 * g + j
                    dve(lambda v, j=j, h_=h_: v.tensor_scalar(out=dtmp[:, j * 128:(j + 1) * 128],
                                                              in0=PB[6][:, j * 128:(j + 1) * 128],
                                                              scalar1=sexp[:, h_:h_ + 1], scalar2=None, op0=ALU.add),
                        [PBR[6]], [R_dt])
                dve(lambda v: v.reciprocal(out=dtmp[:, :], in_=dtmp[:, :]), [R_dt], [R_dt])
                dve(lambda v, g=g, qb=qb: v.tensor_tensor(
                    out=attn[:, 4 * g:4 * g + 4, qb * 128:(qb + 1) * 128],
                    in0=PB[5][:, :].rearrange("p (h q) -> p h q", h=4),
                    in1=dtmp[:, :].rearrange("p (h q) -> p h q", h=4), op=ALU.mult), [PBR[5], R_dt], [R_attn])
        barrier()
        R_sg = [Res() for _ in range(3)]
        R_mt = [Res() for _ in range(3)]
        R_mx = [Res() for _ in range(KC)]
        wao = kview(w_ao_d[l])
        wa = kview(w_aout_d[l])
        wb = kview(w_bout_d[l])
        for op_ in range(8):
            c0 = op_ * 256
            slabO, RO = wload1(wao[:, :, c0:c0 + 256], [KC, 256])
            slabAB, RAB = wload([(lambda v: v[:, 0:8, :], wa[:, :, c0:c0 + 256]),
                                 (lambda v: v[:, 8:16, :], wb[:, :, c0:c0 + 256])], [KC, 256])
            slabG = [wload1(wv[:, :, 8192 + gi * 2048 + c0:8192 + gi * 2048 + c0 + 256], [KC, 256])
                     for gi in range(3)]
            for j in range(2):
                oc = op_ * 2 + j
                cs_ = slice(j * 128, (j + 1) * 128)
                for gi in range(3):
                    slab, R = slabG[gi]
                    b = nbank()
                    mm([(PB[b][:, :], slab[:, kc, cs_], hb[0][:, kc, :], kc == 0, kc == KC - 1) for kc in range(KC)],
                       [R] + hall, [PBR[b]])
                    act(sgm[gi][:, :], PB[b][:, :], AF.Sigmoid, [PBR[b]], [R_sg[gi]])
                    b2 = nbank()
                    if gi == 0:
                        mm([(PB[b2][:, :], slabAB[:, c, cs_], afeat[:, c, :], c == 0, c == 7) for c in range(8)],
                           [RAB] + R_af, [PBR[b2]])
                    elif gi == 1:
                        mm([(PB[b2][:, :], slabAB[:, 8 + c, cs_], binp[:, c, :], c == 0, c == 7) for c in range(8)],
                           [RAB] + R_bi, [PBR[b2]])
                    else:
                        mm([(PB[b2][:, :], slabO[:, kc, cs_], attn[:, kc, :], kc == 0, kc == KC - 1)
                            for kc in range(KC)], [RO, R_attn], [PBR[b2]])
                    dve(lambda v, b2=b2, gi=gi: v.tensor_tensor(out=mt[gi][:, :], in0=PB[b2][:, :],
                                                                in1=sgm[gi][:, :], op=ALU.mult),
                        [PBR[b2], R_sg[gi]], [R_mt[gi]])
                dve(lambda v: v.tensor_tensor(out=mt[0][:, :], in0=mt[0][:, :], in1=mt[1][:, :], op=ALU.add),
                    [R_mt[0], R_mt[1]], [R_mt[0]])
                dve(lambda v, oc=oc: v.tensor_tensor(out=mixed[:, oc, :], in0=mt[0][:, :], in1=mt[2][:, :],
                                                     op=ALU.add), [R_mt[0], R_mt[2]], [R_mx[oc]])
        wo = kview(w_out_d[l])
        for op_ in range(8):
            slab, R = wload1(wo[:, :, op_ * 256:(op_ + 1) * 256], [KC, 256])
            for j in range(2):
                oc = op_ * 2 + j
                b = nbank()
                mm([(PB[b][:, :], slab[:, kc, j * 128:(j + 1) * 128], mixed[:, kc, :], kc == 0, kc == KC - 1)
                    for kc in range(KC)], [R] + R_mx, [PBR[b]])
                dve(lambda v, b=b, oc=oc: v.scalar_tensor_tensor(
                    out=xb[0][:, oc, :], in0=PB[b][:, :], scalar=Gsc(l, 1, oc, v_), in1=xb[0][:, oc, :],
                    op0=ALU.mult, op1=ALU.add), [PBR[b]], [R_x[0][oc]])
        store_x(0, tile)
        barrier()

    ctx_sem = es.enter_context(nc.semaphore("ctx"))
    ctx_val = [0]

    def ctx_load(l, R_kc, R_vc):
        GP.deps([], [R_kc, R_vc])
        GP.wait_ev(PE.last())
        GP.wait_ev(ACT.last())
        GP.wait_ev(DVE.last())
        nc.gpsimd.dma_start(out=kctx[:, :, :], in_=kctxT_d[l]).then_inc(ctx_sem, 16)
        nc.gpsimd.dma_start(out=vctx[:, :, :], in_=vctx_d[l].rearrange("(b p) d -> p b d", p=128)).then_inc(ctx_sem, 16)
        ctx_val[0] += 32
        ev = Ev(ctx_sem, ctx_val[0])
        R_kc.set_writer(ev)
        R_vc.set_writer(ev)

    def final_norm(t, tile):
        sumsq([(xb[t][:, kc, :], R_x[t][kc]) for kc in range(KC)])
        rstd_from_bank7(D)
        for kc in range(KC):
            dve(lambda v, kc=kc: v.scalar_tensor_tensor(
                out=xb[t][:, kc, :], in0=xb[t][:, kc, :], scalar=gfin[:, kc:kc + 1], in1=rstd[:, :],
                op0=ALU.mult, op1=ALU.mult), [R_x[t][kc], R_rstd], [R_x[t][kc]])
        rx = [R_x[t][kc] for kc in range(KC)]
        if tile == 0:
            SQ.dma(yT_o[:, :, 0:512], xb[t][:, :, :], rx, [])
        elif tile == 1:
            SQ.dma(yT_o[:, :, 512:768], xb[t][:, :, 256:512], rx, [])
        elif tile == 2:
            SQ.dma(yT_o[:, :, 768:1280], xb[t][:, :, :], rx, [])
        else:
            SQ.dma(yT_o[:, :, 1280:1536], xb[t][:, :, 0:256], rx, [])

    def stop(tag):
        if STOP[0] == tag:
            raise _Stop()

    def _program():
        STS = [(0, 1), (2, 3)]
        stop("ada")
        for st in range(2):
            tiles = STS[st]
            for t in range(2):
                load_x(xT_d, t, tiles[t] * 512)
            stop("load")
            ffn(0, 0, tiles)
            stop("ffn")
            m1(0, tiles)
            stop("m1")
            for t in range(2):
                store_x(t, tiles[t])
            barrier()
        stop("ph1")
        for l in range(NL):
            for tile in range(4):
                ph2(l, tile)
                stop("ph2_%d_%d" % (l, tile))
            for st in range(2):
                tiles = STS[st]
                for t in range(2):
                    load_x(xs_s, t, tiles[t] * 512)
                ffn(l, 1, tiles)
                stop("ffn2_%d_%d" % (l, st))
                if l + 1 < NL:
                    ffn(l + 1, 0, tiles)
                    m1(l + 1, tiles)
                    for t in range(2):
                        store_x(t, tiles[t])
                else:
                    for t in range(2):
                        final_norm(t, tiles[t])
                barrier()

    try:
        _program()
    except _Stop:
        barrier()
        dbg_o = dout("dbg", [128, 2, KC, 512])
        dbh_o = dout("dbh", [128, 2, KC, 512])
        R_dbg = Res()
        for t in range(2):
            SQ.dma(dbg_o[:, t, :, :], xb[t][:, :, :], [R_dbg], [])
        for t in range(2):
            act(xb[t][:, :, :], hb[t][:, :, :], AF.Copy, [], [R_dbg])
        for E in (SP,):
            E.wait_ev(ACT.last())
        for t in range(2):
            SQ.dma(dbh_o[:, t, :, :], xb[t][:, :, :], [], [])
    for e in SQ.all_last():
        SP.wait_ev(e)
    es.close()
    return nc


def _fm(x2d):
    T, Fd = x2d.shape
    return np.ascontiguousarray(x2d.T.reshape(Fd // 128, 128, T).transpose(1, 0, 2))


def _vec_fm(v):
    sh = v.shape
    Fd = sh[-1]
    r = v.reshape(sh[:-1] + (Fd // 128, 128))
    return np.ascontiguousarray(np.moveaxis(r, -1, 0))


_CACHE = {}
_RETURN_MAPS = [False]


def kernel(x_prompt, x_sample, cache_k, cache_v, c, c_ctx, w_ada, b_ada, g_ff1, w_ff1_in, w_ff1_out, g_mix,
           w_in, attn_sink, w_attn_o, conv_a_w, conv_a_b, g_conv_a, w_a_out, conv_b_w, w_b_out, w_out, g_ff2,
           w_ff2_in, w_ff2_out, g_final):
    f32 = np.float32
    A = lambda a: np.ascontiguousarray(np.asarray(a, dtype=f32))
    x_prompt, x_sample, cache_k, cache_v, c, c_ctx = map(A, (x_prompt, x_sample, cache_k, cache_v, c, c_ctx))
    shared = {
        "w_ada": A(w_ada), "w_ff1_in": A(w_ff1_in), "w_ff1_out": A(w_ff1_out), "w_in": A(w_in),
        "w_attn_o": A(w_attn_o), "w_a_out": A(w_a_out), "w_b_out": A(w_b_out), "w_out": A(w_out),
        "w_ff2_in": A(w_ff2_in), "w_ff2_out": A(w_ff2_out),
    }
    shared["badaT"] = _vec_fm(A(b_ada))
    g3 = np.stack([A(g_ff1), A(g_mix), A(g_ff2)], axis=1)
    shared["gT"] = _vec_fm(g3)
    shared["gfin"] = _vec_fm(A(g_final))
    caw = A(conv_a_w)[:, :, 0, :]
    shared["caw"] = np.ascontiguousarray(_vec_fm(caw).transpose(0, 1, 3, 2))
    shared["cab"] = _vec_fm(A(conv_a_b))
    shared["gca"] = _vec_fm(A(g_conv_a))
    cbw = A(conv_b_w)[:, :, 0, :]
    shared["cbw"] = np.ascontiguousarray(_vec_fm(cbw).transpose(0, 1, 3, 2))
    shared["sinkb"] = np.ascontiguousarray(np.broadcast_to(A(attn_sink)[None], (128, NL, 16)))
    shared["onesm"] = np.ones((128, 128), f32)
    Rt = np.zeros((128, 128), f32)
    for base in (0, 64):
        for p in range(32):
            Rt[base + p + 32, base + p] = -1.0
            Rt[base + p, base + p + 32] = 1.0
    shared["Rt"] = Rt
    kl = np.arange(128)[:, None]
    ql = np.arange(128)[None, :]
    tri = np.stack([np.tile((kl >= ql).astype(f32), (1, 4)), np.tile((kl <= ql).astype(f32), (1, 4))], axis=1)
    shared["tri"] = np.ascontiguousarray(tri)
    inv = (np.float32(10000.0) ** (-np.arange(32, dtype=f32) / np.float32(32))).astype(f32)

    in_maps = []
    for i in range(NCORES):
        b = i // 4
        s = (i % 4) * 1024
        gp = s - 256 + np.arange(REG)
        ok = (gp >= 0) & (gp < 4096)
        xs_ = np.zeros((REG, D), f32)
        xs_[ok] = x_sample[b, gp[ok]]
        X = np.concatenate([x_prompt[2 * i], x_prompt[2 * i + 1], xs_], axis=0)
        m = dict(shared)
        m["xT"] = _fm(X)
        m["cT"] = np.ascontiguousarray(np.stack([_vec_fm(c_ctx), _vec_fm(c[b])], axis=-1))
        m["kctxT"] = np.ascontiguousarray(cache_k[b].transpose(0, 3, 2, 1))
        m["vctx"] = np.ascontiguousarray(cache_v[b].reshape(NL, 512, 512))
        gpc = np.clip(gp, 0, 4095)
        row = (gpc // 64).astype(f32)
        col = (gpc % 64).astype(f32)
        ang_r = (row[:, None] * inv[None, :]).astype(f32)
        ang_c = (col[:, None] * inv[None, :]).astype(f32)
        cr, sr = np.cos(ang_r).astype(f32), np.sin(ang_r).astype(f32)
        cc, sc_ = np.cos(ang_c).astype(f32), np.sin(ang_c).astype(f32)
        m["cosT"] = np.ascontiguousarray(np.concatenate([cr, cr, cc, cc], axis=1).T)
        m["sinT"] = np.ascontiguousarray(np.concatenate([sr, sr, sc_, sc_], axis=1).T)
        m["valid"] = np.ascontiguousarray(np.broadcast_to(ok.astype(f32)[None], (128, REG)))
        kb = np.where(ok, 0.0, NEG).astype(f32).reshape(12, 128).T
        m["keybias"] = np.ascontiguousarray(kb)
        in_maps.append(m)

    if _RETURN_MAPS[0]:
        return in_maps
    if "nc" not in _CACHE:
        _CACHE["nc"] = build_program()
    nc = _CACHE["nc"]
    res = run_bass_kernel_spmd(nc, in_maps, core_ids=list(range(NCORES)))
    y_prompt = np.zeros((16, 256, D), f32)
    y_sample = np.zeros((2, 4096, D), f32)
    nk = np.zeros((16, NL, 256, 4, 128), f32)
    nv = np.zeros((16, NL, 256, 4, 128), f32)
    for i in range(NCORES):
        r = res.results[i]
        yT = np.asarray(r["yT"])
        Y = yT.transpose(2, 1, 0).reshape(1536, D)
        y_prompt[2 * i] = Y[0:256]
        y_prompt[2 * i + 1] = Y[256:512]
        b = i // 4
        s = (i % 4) * 1024
        y_sample[b, s:s + 1024] = Y[512:1536]
        kTo = np.asarray(r["kTo"])
        vo = np.asarray(r["vo"])
        for sq_ in range(2):
            nk[2 * i + sq_] = kTo[:, :, :, sq_ * 256:(sq_ + 1) * 256].transpose(0, 3, 2, 1)
            nv[2 * i + sq_] = vo[:, sq_ * 256:(sq_ + 1) * 256, :].reshape(NL, 256, 4, 128)
    return (y_prompt, y_sample, nk, nv)
```

```python
import numpy as np
from contextlib import ExitStack
import concourse.bass as bass
import concourse.mybir as mybir
from concourse.bass_utils import run_bass_kernel_spmd

F32 = mybir.dt.float32
BF16 = mybir.dt.bfloat16
U8 = mybir.dt.uint8
AF = mybir.ActivationFunctionType
ALU = mybir.AluOpType

D = 2048
KC = 16
DFF = 5632
NL = 2
TS = 512
NTOK = 2048
REG = 1536
NCORES = 8
KOFF = 640
KL = KOFF + REG + 128
AOFF = 560
AL = AOFF + REG + 16
APOS = (16, 288)
EPS = 1e-6
NEG = -30000.0
SCALE = 128.0 ** -0.5
ARENA = 210944
NSLOT = 6
SLOT = 8192


class Ev:
    __slots__ = ("sem", "val")

    def __init__(self, sem, val):
        self.sem = sem
        self.val = val


class Res:
    __slots__ = ("w", "r", "const", "excl")

    def __init__(self, const=False, excl=False):
        self.w = None
        self.r = {}
        self.const = const
        self.excl = excl

    def add_reader(self, ev):
        if self.const:
            return
        k = id(ev.sem)
        o = self.r.get(k)
        if o is None or o.val < ev.val:
            self.r[k] = ev

    def set_writer(self, ev):
        self.w = ev
        self.r = {}


class Eng:
    def __init__(self, nc, es, name, eng):
        self.name = name
        self.eng = eng
        self.sem = es.enter_context(nc.semaphore("p_" + name))
        self.cnt = 0
        self.waited = {}

    def wait_ev(self, ev):
        if ev is None or ev.sem is self.sem:
            return
        k = id(ev.sem)
        if self.waited.get(k, 0) >= ev.val:
            return
        self.eng.wait_ge(ev.sem, ev.val)
        self.waited[k] = ev.val

    def deps(self, reads, writes):
        best = {}

        def add(ev):
            if ev is None or ev.sem is self.sem:
                return
            k = id(ev.sem)
            o = best.get(k)
            if o is None or o.val < ev.val:
                best[k] = ev

        for R in reads:
            add(R.w)
            if R.excl:
                for e in R.r.values():
                    add(e)
        for R in writes:
            add(R.w)
            for e in R.r.values():
                add(e)
        for ev in best.values():
            self.wait_ev(ev)

    def done(self, ins, reads, writes):
        self.cnt += 1
        ins.then_inc(self.sem, 1)
        ev = Ev(self.sem, self.cnt)
        for R in reads:
            R.add_reader(ev)
        for R in writes:
            R.set_writer(ev)
        return ev

    def last(self):
        return Ev(self.sem, self.cnt) if self.cnt else None


class DmaQ:
    def __init__(self, nc, es, E, nsem, name):
        self.E = E
        self.sems = [es.enter_context(nc.semaphore("%s%d" % (name, i))) for i in range(nsem)]
        self.vals = [0] * nsem
        self.i = 0

    def dma(self, out, in_, reads, writes):
        E = self.E
        E.deps(reads, writes)
        k = self.i % len(self.sems)
        self.i += 1
        if self.vals[k]:
            E.wait_ev(Ev(self.sems[k], self.vals[k]))
        self.vals[k] += 16
        E.eng.dma_start(out=out, in_=in_).then_inc(self.sems[k], 16)
        ev = Ev(self.sems[k], self.vals[k])
        for R in reads:
            R.add_reader(ev)
        for R in writes:
            R.set_writer(ev)
        return ev

    def all_last(self):
        return [Ev(s, v) for s, v in zip(self.sems, self.vals) if v]


STOP = [None]
SKIP = set()


class _Stop(Exception):
    pass


def build_program():
    nc = bass.Bass("TRN2", target_bir_lowering=False)
    es = ExitStack()

    def din(name, shape):
        if "now" in SKIP and name.startswith("w_"):
            return nc.dram_tensor(name, [1] + list(shape)[1:], F32, kind="Internal").ap()
        return nc.dram_tensor(name, list(shape), F32, kind="ExternalInput").ap()

    def dout(name, shape):
        return nc.dram_tensor(name, list(shape), F32, kind="ExternalOutput").ap()

    def dscr(name, shape, dt):
        return nc.dram_tensor(name, list(shape), dt, kind="Internal").ap()

    xT_d = din("xT", [128, KC, NTOK])
    cT_d = din("cT", [128, KC, 2])
    w_ada_d = din("w_ada", [NL, D, 9 * D])
    badaT_d = din("badaT", [128, NL, 144])
    gT_d = din("gT", [128, NL, 3, KC])
    gfin_d = din("gfin", [128, KC])
    wffin_d = [din("w_ff1_in", [NL, D, 2 * DFF]), din("w_ff2_in", [NL, D, 2 * DFF])]
    wffout_d = [din("w_ff1_out", [NL, DFF, D]), din("w_ff2_out", [NL, DFF, D])]
    w_in_d = din("w_in", [NL, D, 14336])
    w_ao_d = din("w_attn_o", [NL, D, D])
    w_aout_d = din("w_a_out", [NL, 1024, D])
    w_bout_d = din("w_b_out", [NL, 1024, D])
    w_out_d = din("w_out", [NL, D, D])
    caw_d = din("caw", [128, NL, 8, 31])
    cab_d = din("cab", [128, NL, 8])
    gca_d = din("gca", [128, NL, 8])
    cbw_d = din("cbw", [128, NL, 8, 3])
    sinkb_d = din("sinkb", [128, NL, 16])
    kctxT_d = din("kctxT", [NL, 128, 4, 512])
    vctx_d = din("vctx", [NL, 512, 512])
    cosT_d = din("cosT", [128, REG])
    sinT_d = din("sinT", [128, REG])
    valid_d = din("valid", [128, REG])
    keybias_d = din("keybias", [128, 12])
    onesm_d = din("onesm", [128, 128])
    Rt_d = din("Rt", [128, 128])
    tri_d = din("tri", [128, 2, 512])

    yT_o = dout("yT", [128, KC, 1536])
    kTo_o = dout("kTo", [NL, 128, 4, 512])
    vo_o = dout("vo", [NL, 512, 512])

    xs_s = dscr("xs", [128, KC, NTOK], F32)
    ksp_s = dscr("ksp", [128, 4, KL], BF16)
    vsp_s = dscr("vsp", [KL, 512], BF16)
    asp_s = dscr("asp", [128, 8, AL], BF16)
    cxsp_s = dscr("cxsp", [128, 8, AL], BF16)
    bgsp_s = dscr("bgsp", [128, 8, NTOK], BF16)

    def kview(w2d):
        return w2d.rearrange("(kc p) n -> p kc n", p=128)

    PE = Eng(nc, es, "pe", nc.tensor)
    ACT = Eng(nc, es, "act", nc.scalar)
    DVE = Eng(nc, es, "dve", nc.vector)
    GP = Eng(nc, es, "gp", nc.gpsimd)
    SP = Eng(nc, es, "sp", nc.sync)
    SQ = DmaQ(nc, es, SP, 8, "sq")

    arena = es.enter_context(nc.sbuf_tensor("arena", [128, ARENA], U8))
    off = [0]

    def carve_at(o, shape, dt):
        nb = 4 if dt == F32 else 2
        n = int(np.prod(shape)) * nb
        v = arena[:, o:o + n].bitcast(dt)
        if len(shape) == 2:
            v = v.rearrange("p (a b) -> p a b", a=shape[0])
        elif len(shape) == 3:
            v = v.rearrange("p (a b c) -> p a b c", a=shape[0], b=shape[1])
        return v, o + ((n + 63) // 64) * 64

    def carve(shape, dt):
        v, off[0] = carve_at(off[0], shape, dt)
        return v

    ringb = carve([NSLOT, SLOT // 2], BF16)
    modsb = carve([NL, 144, 2], F32)
    Atab = carve([NL * 3 * KC * 2], F32)
    Gtab = carve([NL * 3 * KC * 2], F32)
    badaT = carve([NL, 144], F32)
    gT = carve([NL * 3, KC], F32)
    gfin = carve([KC], F32)
    caw = carve([NL * 8, 31], F32)
    cab = carve([NL * 8], F32)
    gca = carve([NL * 8], F32)
    cbw = carve([NL * 8, 3], F32)
    sinkb = carve([NL * 16], F32)
    sexp = carve([NL * 16], F32)
    cTf = carve([KC, 2], F32)
    csil = carve([KC, 2], BF16)
    ones = carve([128], BF16)
    Rt = carve([128], BF16)
    tri = carve([2, 512], BF16)
    ident = carve([128], BF16)
    keybias = carve([12], F32)
    valid = carve([REG], BF16)
    cosb = [carve([512], F32) for _ in range(2)]
    sinb = [carve([512], F32) for _ in range(2)]
    sqb = [carve([512], BF16) for _ in range(3)]
    rstd = carve([512], F32)
    ntmp = [carve([512], F32) for _ in range(3)]
    xb = [carve([KC, 512], F32), None]
    hb = [carve([KC, 512], BF16), None]
    PH0 = off[0]
    xb[1] = carve([KC, 512], F32)
    hb[1] = carve([KC, 512], BF16)
    PHT = off[0]
    assert ARENA - PHT >= 24576, (ARENA - PHT)
    o = PHT
    hid = []
    for i in range(2):
        v, o = carve_at(o, [2, 1024], BF16)
        hid.append(v)
    su = []
    for i in range(3):
        v, o = carve_at(o, [512], BF16)
        su.append(v)
    stb = []
    for i in range(4):
        v, o = carve_at(o, [512], BF16)
        stb.append(v)
    stf = []
    for i in range(2):
        v, o = carve_at(o, [512], F32)
        stf.append(v)
    rxb1, o = carve_at(o, [512], BF16)
    rt1a, o = carve_at(o, [512], F32)
    rt2a, o = carve_at(o, [512], F32)
    assert o <= ARENA, o
    o = PH0
    attn, o = carve_at(o, [KC, 512], BF16)
    afeat, o = carve_at(o, [8, 512], BF16)
    binp, o = carve_at(o, [8, 512], BF16)
    T0 = o
    awin, o = carve_at(T0, [8, 576], BF16)
    acv, o = carve_at(o, [8, 512], F32)
    dg = []
    for i in range(2):
        v, o = carve_at(o, [31, 128], BF16)
        dg.append(v)
    assert o <= ARENA, o
    cxwin, o = carve_at(T0, [8, 544], BF16)
    bgwin, o = carve_at(o, [8, 512], BF16)
    cacc, o = carve_at(o, [512], F32)
    assert o <= ARENA
    qg, o = carve_at(T0, [4, 512], BF16)
    kwin, o = carve_at(o, [4, 768], BF16)
    vwin, o = carve_at(o, [6, 512], BF16)
    kctx, o = carve_at(o, [4, 512], BF16)
    vctx, o = carve_at(o, [4, 512], BF16)
    pT = []
    for i in range(3):
        v, o = carve_at(o, [512], BF16)
        pT.append(v)
    dtmp, o = carve_at(o, [512], F32)
    rxb2 = []
    rt1b = []
    rt2b = []
    for i in range(2):
        v, o = carve_at(o, [512], BF16)
        rxb2.append(v)
        v, o = carve_at(o, [512], F32)
        rt1b.append(v)
        v, o = carve_at(o, [512], F32)
        rt2b.append(v)
    assert o <= ARENA, o
    mixed, o = carve_at(T0, [KC, 512], BF16)
    sgm = []
    for i in range(3):
        v, o = carve_at(o, [512], BF16)
        sgm.append(v)
    mt = []
    for i in range(3):
        v, o = carve_at(o, [512], F32)
        mt.append(v)
    assert o <= ARENA, o

    PB = [es.enter_context(nc.psum_tensor("pb%d" % i, [128, 512], F32)) for i in range(8)]
    PBR = [Res(excl=True) for _ in range(8)]
    rot = [0]

    pool = [5]

    def nbank():
        i = rot[0] % pool[0]
        rot[0] += 1
        return i

    def act(out, in_, func, reads, writes, bias=None, scale=None):
        ACT.deps(reads, writes)
        kw = {}
        if bias is not None:
            kw["bias"] = bias
        if scale is not None:
            kw["scale"] = scale
        ins = nc.scalar.activation(out=out, in_=in_, func=func, **kw)
        return ACT.done(ins, reads, writes)

    def dve(fn, reads, writes):
        DVE.deps(reads, writes)
        ins = fn(nc.vector)
        return DVE.done(ins, reads, writes)

    def mm(specs, reads, writes):
        PE.deps(reads, writes)
        ins = None
        for (o_, l_, r_, st_, sp_) in specs:
            ins = nc.tensor.matmul(o_, lhsT=l_, rhs=r_, start=st_, stop=sp_)
        return PE.done(ins, reads, writes)

    def barrier():
        evs = [PE.last(), ACT.last(), DVE.last()] + SQ.all_last()
        for E in (PE, ACT, DVE, SP):
            for e in evs:
                E.wait_ev(e)

    ring_sems = [es.enter_context(nc.semaphore("ring%d" % i)) for i in range(NSLOT)]
    ring_vals = [0] * NSLOT
    ring_res = [Res() for _ in range(NSLOT)]
    ring_n = [0]

    def wload(parts, shape):
        s = ring_n[0] % NSLOT
        ring_n[0] += 1
        R = ring_res[s]
        GP.deps([], [R])
        n = int(np.prod(shape))
        v = ringb[:, s, 0:n]
        if len(shape) == 2:
            v = v.rearrange("p (a b) -> p a b", a=shape[0])
        for dstf, src in parts:
            nc.gpsimd.dma_start(out=dstf(v), in_=src).then_inc(ring_sems[s], 16)
            ring_vals[s] += 16
        R.set_writer(Ev(ring_sems[s], ring_vals[s]))
        return v, R

    def wload1(src, shape):
        return wload([(lambda v: v, src)], shape)

    def gload(dst, src, R):
        raise NotImplementedError

    setup_sem = [es.enter_context(nc.semaphore("setup_hw")), es.enter_context(nc.semaphore("setup_sw"))]
    setup_n = [0, 0]

    def sload(dst, src, cast=False):
        eng = nc.gpsimd if cast else nc.sync
        i = 1 if cast else 0
        eng.dma_start(out=dst, in_=src).then_inc(setup_sem[i], 16)
        setup_n[i] += 16

    sload(badaT, badaT_d)
    sload(gT, gT_d.rearrange("p l s k -> p (l s) k"))
    sload(gfin, gfin_d)
    sload(caw, caw_d.rearrange("p l c j -> p (l c) j"))
    sload(cab, cab_d.rearrange("p l c -> p (l c)"))
    sload(gca, gca_d.rearrange("p l c -> p (l c)"))
    sload(cbw, cbw_d.rearrange("p l c j -> p (l c) j"))
    sload(sinkb, sinkb_d.rearrange("p l h -> p (l h)"))
    sload(cTf, cT_d)
    sload(keybias, keybias_d)
    sload(ones, onesm_d, cast=True)
    sload(Rt, Rt_d, cast=True)
    sload(tri, tri_d, cast=True)
    sload(valid, valid_d, cast=True)
    setup_evs = [Ev(setup_sem[0], setup_n[0]), Ev(setup_sem[1], setup_n[1])]
    R_zero = Res()
    dve(lambda v: v.memset(stb[0][:, 0:128], 0.0), [], [R_zero])
    R_asp = Res()
    R_cxsp = Res()
    zsrc = stb[0][:, 0:128].rearrange("p (c j) -> p c j", c=8)
    for p0 in (0, 272, 544):
        SQ.dma(asp_s[:, :, p0:p0 + 16], zsrc, [R_zero], [])
        SQ.dma(cxsp_s[:, :, p0:p0 + 16], zsrc, [R_zero], [])
    for e in SQ.all_last():
        R_asp.add_reader(e)
    R_attn = Res()
    dve(lambda v: v.memset(attn[:, :, :], 0.0), [], [R_attn])
    for E in (PE, ACT, DVE, SP):
        for e_ in setup_evs:
            E.wait_ev(e_)
    barrier()

    R_id = Res()
    dve(lambda v: v.tensor_tensor(out=ident[:, :], in0=tri[:, 0, 0:128], in1=tri[:, 1, 0:128], op=ALU.mult), [], [R_id])
    for E in (PE, ACT, SP):
        E.wait_ev(R_id.w)
    RC = Res(const=True)

    R_csil = Res()
    act(csil[:, :, :], cTf[:, :, :], AF.Silu, [RC], [R_csil])
    R_mods = Res()
    for l in range(1):
        wv = kview(w_ada_d[l if 'now' not in SKIP else 0])
        for s2 in range(72 if "ada" not in SKIP else 1):
            slab, R = wload1(wv[:, :, s2 * 256:(s2 + 1) * 256], [KC, 256])
            specs = []
            for j in range(2):
                n = s2 * 2 + j
                for kc in range(KC):
                    specs.append((PB[7][:, 2 * n:2 * n + 2], slab[:, kc, j * 128:(j + 1) * 128],
                                  csil[:, kc, :], kc == 0, kc == KC - 1))
            mm(specs, [R, R_csil], [PBR[7]])
        pv = PB[7][:, 0:288].rearrange("p (n v) -> p n v", v=2)
        for v_ in range(2):
            dve(lambda v, v_=v_: v.tensor_tensor(out=modsb[:, l, :, v_], in0=pv[:, :, v_], in1=badaT[:, l, :],
                                                 op=ALU.add), [PBR[7]], [R_mods])
    A3 = Atab.rearrange("p (q k v) -> p q k v", k=KC, v=2)
    G3 = Gtab.rearrange("p (q k v) -> p q k v", k=KC, v=2)

    def derive_tables(l):
        for s_ in range(3):
            for v_ in range(2):
                dve(lambda v, l=l, s_=s_, v_=v_: v.scalar_tensor_tensor(
                    out=A3[:, l * 3 + s_, :, v_], in0=modsb[:, l, (3 * s_ + 1) * 16:(3 * s_ + 2) * 16, v_],
                    scalar=1.0, in1=gT[:, l * 3 + s_, :], op0=ALU.add, op1=ALU.mult), [R_mods], [R_mods])
                dve(lambda v, l=l, s_=s_, v_=v_: v.tensor_scalar(
                    out=G3[:, l * 3 + s_, :, v_], in0=modsb[:, l, (3 * s_ + 2) * 16:(3 * s_ + 3) * 16, v_],
                    scalar1=(1.0 if s_ == 1 else 0.5), scalar2=None, op0=ALU.mult), [R_mods], [R_mods])

    ada1_next = [0]

    def ada1_slabs(n):
        items = []
        wv1 = kview(w_ada_d[1 if 'now' not in SKIP else 0])
        for i in range(n):
            s2 = ada1_next[0]
            if s2 >= 72:
                break
            ada1_next[0] += 1
            slab, R = wload1(wv1[:, :, s2 * 256:(s2 + 1) * 256], [KC, 256])
            specs = []
            for j in range(2):
                for kc in range(KC):
                    specs.append((PB[6][:, 4 * i + 2 * j:4 * i + 2 * j + 2], slab[:, kc, j * 128:(j + 1) * 128],
                                  csil[:, kc, :], kc == 0, kc == KC - 1))
            mm(specs, [R, R_csil], [PBR[6]])
            items.append((s2, 4 * i))
        return items

    def ada1_evac(items):
        for (s2, c0) in items:
            pv1 = PB[6][:, c0:c0 + 4].rearrange("p (n v) -> p n v", v=2)
            for v_ in range(2):
                dve(lambda v, v_=v_, s2=s2, pv1=pv1: v.tensor_tensor(
                    out=modsb[:, 1, 2 * s2:2 * s2 + 2, v_], in0=pv1[:, :, v_], in1=badaT[:, 1, 2 * s2:2 * s2 + 2],
                    op=ALU.add), [PBR[6]], [R_mods])
        if ada1_next[0] >= 72 and items:
            derive_tables(1)

    for l in range(1):
        for s_ in range(3):
            for v_ in range(2):
                dve(lambda v, l=l, s_=s_, v_=v_: v.scalar_tensor_tensor(
                    out=A3[:, l * 3 + s_, :, v_], in0=modsb[:, l, (3 * s_ + 1) * 16:(3 * s_ + 2) * 16, v_],
                    scalar=1.0, in1=gT[:, l * 3 + s_, :], op0=ALU.add, op1=ALU.mult), [R_mods], [R_mods])
                dve(lambda v, l=l, s_=s_, v_=v_: v.tensor_scalar(
                    out=G3[:, l * 3 + s_, :, v_], in0=modsb[:, l, (3 * s_ + 2) * 16:(3 * s_ + 3) * 16, v_],
                    scalar1=(1.0 if s_ == 1 else 0.5), scalar2=None, op0=ALU.mult), [R_mods], [R_mods])
    act(sexp[:, :], sinkb[:, :], AF.Exp, [RC], [R_mods])
    for E in (PE, ACT, DVE, SP):
        E.wait_ev(R_mods.w)
    def Asc(l, s_, kc, v_):
        i = ((l * 3 + s_) * KC + kc) * 2 + v_
        return Atab[:, i:i + 1]

    def Gsc(l, s_, kc, v_):
        i = ((l * 3 + s_) * KC + kc) * 2 + v_
        return Gtab[:, i:i + 1]

    def Bsc(l, s_, kc, v_):
        return modsb[:, l, 3 * s_ * 16 + kc, v_:v_ + 1]

    R_x = [[Res() for _ in range(KC)] for _ in range(2)]
    R_h = [[Res() for _ in range(KC)] for _ in range(2)]
    R_sq = [Res() for _ in range(3)]
    R_rstd = Res()
    R_nt = [Res() for _ in range(3)]
    R_hid = [[[Res() for _ in range(2)] for _ in range(2)] for _ in range(2)]
    R_su = [Res() for _ in range(3)]
    R_stb = [Res() for _ in range(4)]
    R_stf = [Res() for _ in range(2)]
    R_cs = [Res() for _ in range(2)]
    R_xs = [Res() for _ in range(4)]
    R_ksp = [Res() for _ in range(4)]
    R_vsp = [Res() for _ in range(4)]
    R_aspt = [Res() for _ in range(4)]
    R_cxt = [Res() for _ in range(4)]
    R_bgt = [Res() for _ in range(4)]
    R_out = Res()
    cnt = {"sq": 0, "nt": 0, "su": 0, "stb": 0, "stf": 0}

    def rr(name, n):
        i = cnt[name] % n
        cnt[name] += 1
        return i

    def tile_vec(tile):
        return 0 if tile == 0 else 1

    def rstd_from_bank7(nparts):
        dve(lambda v: v.tensor_scalar(out=rstd[:, :], in0=PB[7][:, :], scalar1=1.0 / nparts, scalar2=EPS,
                                      op0=ALU.mult, op1=ALU.add), [PBR[7]], [R_rstd])
        act(rstd[:, :], rstd[:, :], AF.Sqrt, [R_rstd], [R_rstd])
        dve(lambda v: v.reciprocal(out=rstd[:, :], in_=rstd[:, :]), [R_rstd], [R_rstd])

    def sumsq(srcs):
        n = len(srcs)
        for i, (ap, R) in enumerate(srcs):
            b = rr("sq", 3)
            act(sqb[b][:, :], ap, AF.Square, [R], [R_sq[b]])
            mm([(PB[7][:, :], ones[:, :], sqb[b][:, :], i == 0, i == n - 1)], [R_sq[b]], [PBR[7]])

    def norm_mod(l, s_, t, v_):
        sumsq([(xb[t][:, kc, :], R_x[t][kc]) for kc in range(KC)])
        rstd_from_bank7(D)
        for kc in range(KC):
            b = rr("nt", 3)
            dve(lambda v, kc=kc, b=b: v.scalar_tensor_tensor(
                out=ntmp[b][:, :], in0=xb[t][:, kc, :], scalar=Asc(l, s_, kc, v_), in1=rstd[:, :],
                op0=ALU.mult, op1=ALU.mult), [R_x[t][kc], R_rstd], [R_nt[b]])
            act(hb[t][:, kc, :], ntmp[b][:, :], AF.Identity, [R_nt[b]], [R_h[t][kc]],
                bias=Bsc(l, s_, kc, v_), scale=1.0)

    def ffn(l, which, tiles):
        if "ffn" in SKIP:
            return
        s_ = 0 if which == 0 else 2
        nt_ = len(tiles)
        for t in range(nt_):
            norm_mod(l, s_, t, tile_vec(tiles[t]))
        win = kview(wffin_d[which][l])
        wout = kview(wffout_d[which][l])
        allh = [R_h[t][kc] for t in range(nt_) for kc in range(KC)]

        def in_units(g):
            hbuf = g % 2
            slabU, RU = wload1(win[:, :, g * 256:(g + 1) * 256], [KC, 256])
            slabV, RV = wload1(win[:, :, DFF + g * 256:DFF + (g + 1) * 256], [KC, 256])
            units = []
            for j in range(2):
                for t in range(nt_):
                    units.append(lambda j=j, t=t: in_unit(hbuf, slabU, RU, slabV, RV, j, t))
            return units

        def in_unit(hbuf, slabU, RU, slabV, RV, j, t):
            if True:
                if True:
                    bu = nbank()
                    mm([(PB[bu][:, :], slabU[:, kc, j * 128:(j + 1) * 128], hb[t][:, kc, :], kc == 0, kc == KC - 1)
                        for kc in range(KC)], [RU] + [R_h[t][kc] for kc in range(KC)], [PBR[bu]])
                    si = rr("su", 3)
                    act(su[si][:, :], PB[bu][:, :], AF.Silu, [PBR[bu]], [R_su[si]])
                    bv = nbank()
                    mm([(PB[bv][:, :], slabV[:, kc, j * 128:(j + 1) * 128], hb[t][:, kc, :], kc == 0, kc == KC - 1)
                        for kc in range(KC)], [RV] + [R_h[t][kc] for kc in range(KC)], [PBR[bv]])
                    dve(lambda v, bv=bv, si=si, j=j, t=t: v.tensor_tensor(
                        out=hid[hbuf][:, j, t * 512:(t + 1) * 512], in0=PB[bv][:, :], in1=su[si][:, :],
                        op=ALU.mult), [PBR[bv], R_su[si]], [R_hid[hbuf][j][t]])

        def out_units(g):
            hbuf = g % 2
            slabO, RO = wload1(wout[:, 2 * g:2 * g + 2, :], [2, 2048])
            return [lambda q=q: out_unit(hbuf, slabO, RO, range(4 * q, 4 * q + 4)) for q in range(4)]

        def out_unit(hbuf, slabO, RO, ocs):
            for oc in ocs:
                for t in range(nt_):
                    b = nbank()
                    mm([(PB[b][:, :], slabO[:, j, oc * 128:(oc + 1) * 128], hid[hbuf][:, j, t * 512:(t + 1) * 512],
                         j == 0, j == 1) for j in range(2)],
                       [RO, R_hid[hbuf][0][t], R_hid[hbuf][1][t]], [PBR[b]])
                    dve(lambda v, b=b, oc=oc, t=t: v.scalar_tensor_tensor(
                        out=xb[t][:, oc, :], in0=PB[b][:, :], scalar=Gsc(l, s_, oc, tile_vec(tiles[t])),
                        in1=xb[t][:, oc, :], op0=ALU.mult, op1=ALU.add), [PBR[b]], [R_x[t][oc]])

        NG = DFF // 256
        pool[0] = 7
        for u in in_units(0):
            u()
        for g in range(NG):
            outs = out_units(g)
            ins = in_units(g + 1) if g + 1 < NG else []
            for i in range(4):
                if ins:
                    ins[i]()
                outs[i]()
        pool[0] = 5

    def rope(bank, cs, out_ap, R_outs, xbuf_, t1_, t2_, Rtmp):
        Rxb, Rt1, Rt2 = Rtmp
        if "altdst" in SKIP:
            act(stb[3][:, :], PB[bank][:, :], AF.Copy, [PBR[bank]], [Rxb])
        elif "noact" not in SKIP:
            act(xbuf_[:, :], PB[bank][:, :], AF.Copy, [PBR[bank]], [Rxb])
        if "norot" in SKIP:
            b2 = bank
        else:
            b2 = nbank()
            mm([(PB[b2][:, :], Rt[:, :], xbuf_[:, :], True, True)], [Rxb], [PBR[b2]])
        dve(lambda v: v.tensor_tensor(out=t1_[:, :], in0=PB[bank][:, :], in1=cosb[cs][:, :], op=ALU.mult),
            [PBR[bank], R_cs[cs]] + ([Rxb] if "seq" in SKIP else []), [Rt1])
        dve(lambda v: v.tensor_tensor(out=t2_[:, :], in0=PB[b2][:, :], in1=sinb[cs][:, :], op=ALU.mult),
            [PBR[b2], R_cs[cs]], [Rt2])
        dve(lambda v: v.tensor_tensor(out=out_ap, in0=t1_[:, :], in1=t2_[:, :], op=ALU.add),
            [Rt1, Rt2], R_outs)

    def load_cs(cs, tile):
        rp0 = (tile - 1) * 512
        SQ.dma(cosb[cs][:, :], cosT_d[:, rp0:rp0 + 512], [], [R_cs[cs]])
        SQ.dma(sinb[cs][:, :], sinT_d[:, rp0:rp0 + 512], [], [R_cs[cs]])

    R_rm1 = (Res(), Res(), Res())

    def m1(l, tiles):
        nt_ = len(tiles)
        wv = kview(w_in_d[l])
        for t in range(nt_):
            norm_mod(l, 1, t, tile_vec(tiles[t]))
            if tiles[t] > 0:
                load_cs(t, tiles[t])
        hall = [[R_h[t][kc] for kc in range(KC)] for t in range(nt_)]

        def proj(slab, R, j, t):
            b = nbank()
            mm([(PB[b][:, :], slab[:, kc, j * 128:(j + 1) * 128], hb[t][:, kc, :], kc == 0, kc == KC - 1)
                for kc in range(KC)], [R] + hall[t], [PBR[b]])
            return b

        for hp in range(2):
            slab, R = wload1(wv[:, :, 2048 + hp * 256:2048 + (hp + 1) * 256], [KC, 256])
            for j in range(2):
                head = 2 * hp + j
                for t in range(nt_):
                    tile = tiles[t]
                    b = proj(slab, R, j, t)
                    si = rr("stb", 4)
                    if tile == 0:
                        act(stb[si][:, :], PB[b][:, :], AF.Copy, [PBR[b]], [R_stb[si]])
                        fi = rr("stf", 2)
                        dve(lambda v, b=b, fi=fi: v.tensor_copy(out=stf[fi][:, :], in_=PB[b][:, :]),
                            [PBR[b]], [R_stf[fi]])
                        SQ.dma(kTo_o[l, :, head, :], stf[fi][:, :], [R_stf[fi]], [])
                        SQ.dma(ksp_s[:, head, 0:512], stb[si][:, :], [R_stb[si]], [])
                        stop('m1k0')
                    else:
                        rope(b, t, stb[si][:, :], [R_stb[si]], rxb1, rt1a, rt2a, R_rm1)
                        p0 = KOFF + (tile - 1) * 512
                        SQ.dma(ksp_s[:, head, p0:p0 + 512], stb[si][:, :], [R_stb[si]], [])
                        stop('m1k1')
        stop('m1k')
        for half in range(2):
            slab, R = wload1(wv[:, :, 2560 + half * 256:2560 + (half + 1) * 256], [KC, 256])
            for t in range(nt_):
                tile = tiles[t]
                for blk in range(4):
                    b = nbank()
                    mm([(PB[b][:, 0:256], hb[t][:, kc, blk * 128:(blk + 1) * 128], slab[:, kc, :], kc == 0,
                         kc == KC - 1) for kc in range(KC)], [R] + hall[t], [PBR[b]])
                    si = rr("stb", 4)
                    act(stb[si][:, 0:256], PB[b][:, 0:256], AF.Copy, [PBR[b]], [R_stb[si]])
                    if tile == 0:
                        r0 = blk * 128
                        fi = rr("stf", 2)
                        dve(lambda v, b=b, fi=fi: v.tensor_copy(out=stf[fi][:, 0:256], in_=PB[b][:, 0:256]),
                            [PBR[b]], [R_stf[fi]])
                        SQ.dma(vo_o[l, r0:r0 + 128, half * 256:(half + 1) * 256], stf[fi][:, 0:256],
                               [R_stf[fi]], [])
                    else:
                        r0 = KOFF + (tile - 1) * 512 + blk * 128
                    SQ.dma(vsp_s[r0:r0 + 128, half * 256:(half + 1) * 256], stb[si][:, 0:256], [R_stb[si]],
                           [])

        stop('m1v')

        def conv_store(scr, Rscr, c, tile, si):
            if tile == 0:
                for sg_ in range(2):
                    p0 = APOS[sg_]
                    SQ.dma(scr[:, c, p0:p0 + 256], stb[si][:, sg_ * 256:(sg_ + 1) * 256], [R_stb[si]], [])
            else:
                p0 = AOFF + (tile - 1) * 512
                SQ.dma(scr[:, c, p0:p0 + 512], stb[si][:, :], [R_stb[si]], [])

        for cp in range(4):
            slabA, RA = wload1(wv[:, :, 3072 + cp * 256:3072 + (cp + 1) * 256], [KC, 256])
            slabG, RG = wload1(wv[:, :, 4096 + cp * 256:4096 + (cp + 1) * 256], [KC, 256])
            for j in range(2):
                c = 2 * cp + j
                for t in range(nt_):
                    tile = tiles[t]
                    bg_ = proj(slabG, RG, j, t)
                    ui = rr("su", 3)
                    act(su[ui][:, :], PB[bg_][:, :], AF.Sigmoid, [PBR[bg_]], [R_su[ui]])
                    if tile > 0:
                        rp0 = (tile - 1) * 512
                        dve(lambda v, ui=ui, rp0=rp0: v.tensor_tensor(out=su[ui][:, :], in0=su[ui][:, :],
                                                                      in1=valid[:, rp0:rp0 + 512], op=ALU.mult),
                            [R_su[ui]], [R_su[ui]])
                    ba = proj(slabA, RA, j, t)
                    si = rr("stb", 4)
                    dve(lambda v, ba=ba, ui=ui, si=si: v.tensor_tensor(out=stb[si][:, :], in0=PB[ba][:, :],
                                                                       in1=su[ui][:, :], op=ALU.mult),
                        [PBR[ba], R_su[ui]], [R_stb[si]])
                    conv_store(asp_s, R_aspt, c, tile, si)
        stop('m1a')
        for cp in range(4):
            slabB, RB = wload1(wv[:, :, 5120 + cp * 256:5120 + (cp + 1) * 256], [KC, 256])
            slabC, RCg = wload1(wv[:, :, 6144 + cp * 256:6144 + (cp + 1) * 256], [KC, 256])
            slabX, RX = wload1(wv[:, :, 7168 + cp * 256:7168 + (cp + 1) * 256], [KC, 256])
            for j in range(2):
                c = 2 * cp + j
                for t in range(nt_):
                    tile = tiles[t]
                    bb = proj(slabB, RB, j, t)
                    si = rr("stb", 4)
                    act(stb[si][:, :], PB[bb][:, :], AF.Copy, [PBR[bb]], [R_stb[si]])
                    SQ.dma(bgsp_s[:, c, tile * 512:(tile + 1) * 512], stb[si][:, :], [R_stb[si]], [])
                    bx = proj(slabX, RX, j, t)
                    ui = rr("su", 3)
                    act(su[ui][:, :], PB[bx][:, :], AF.Copy, [PBR[bx]], [R_su[ui]])
                    if tile > 0:
                        rp0 = (tile - 1) * 512
                        dve(lambda v, ui=ui, rp0=rp0: v.tensor_tensor(out=su[ui][:, :], in0=su[ui][:, :],
                                                                      in1=valid[:, rp0:rp0 + 512], op=ALU.mult),
                            [R_su[ui]], [R_su[ui]])
                    bc = proj(slabC, RCg, j, t)
                    si = rr("stb", 4)
                    dve(lambda v, bc=bc, ui=ui, si=si: v.tensor_tensor(out=stb[si][:, :], in0=PB[bc][:, :],
                                                                       in1=su[ui][:, :], op=ALU.mult),
                        [PBR[bc], R_su[ui]], [R_stb[si]])
                    conv_store(cxsp_s, R_cxt, c, tile, si)

    def load_x(src, t, c0):
        SQ.dma(xb[t][:, :, :], src[:, :, c0:c0 + 512], [],
               [R_x[t][kc] for kc in range(KC)])

    def store_x(t, tile):
        SQ.dma(xs_s[:, :, tile * 512:(tile + 1) * 512], xb[t][:, :, :], [R_x[t][kc] for kc in range(KC)],
               [])

    R_T = Res()

    def ph2(l, tile):
        sample = tile > 0
        v_ = tile_vec(tile)
        wv = kview(w_in_d[l])
        barrier()
        load_x(xs_s, 0, tile * 512)
        if sample:
            load_cs(0, tile)
        norm_mod(l, 1, 0, v_)
        hall = [R_h[0][kc] for kc in range(KC)]
        rp0 = (tile - 1) * 512
        if sample:
            segs = [(0, 512, 0)]
        else:
            segs = [(0, 256, 0), (256, 256, 288)]
        R_aw = Res()
        if sample:
            SQ.dma(awin[:, :, 0:542], asp_s[:, :, AOFF + rp0 - 15:AOFF + rp0 + 527],
                   [], [R_aw])
        else:
            for sg_ in range(2):
                p0 = APOS[sg_] - 15
                SQ.dma(awin[:, :, sg_ * 288:sg_ * 288 + 286], asp_s[:, :, p0:p0 + 286], [], [R_aw])
        ada_items = ada1_slabs(9) if l == 0 else []
        R_acv = [Res() for _ in range(8)]
        R_dg = [Res(), Res()]
        for c in range(8):
            lc = l * 8 + c
            d_ = dg[c % 2]
            Rd = R_dg[c % 2]
            for j in range(31):
                dve(lambda v, d_=d_, lc=lc, j=j: v.tensor_scalar(
                    out=d_[:, j, :], in0=ident[:, :], scalar1=caw[:, lc, j:j + 1], scalar2=None, op0=ALU.mult),
                    [], [Rd])
            b = nbank()
            for (o0, ln, w0) in segs:
                mm([(PB[b][:, o0:o0 + ln], d_[:, j, :], awin[:, c, w0 + j:w0 + j + ln], j == 0, j == 30)
                    for j in range(31)], [Rd, R_aw], [PBR[b]])
            act(acv[:, c, :], PB[b][:, :], AF.Identity, [PBR[b]], [R_acv[c]], bias=cab[:, lc:lc + 1], scale=1.0)
        sumsq([(acv[:, c, :], R_acv[c]) for c in range(8)])
        rstd_from_bank7(1024)
        R_af = [Res() for _ in range(8)]
        for c in range(8):
            b = rr("nt", 3)
            dve(lambda v, c=c, b=b: v.scalar_tensor_tensor(
                out=ntmp[b][:, :], in0=acv[:, c, :], scalar=gca[:, l * 8 + c:l * 8 + c + 1], in1=rstd[:, :],
                op0=ALU.mult, op1=ALU.mult), [R_acv[c], R_rstd], [R_nt[b]])
            act(afeat[:, c, :], ntmp[b][:, :], AF.Silu, [R_nt[b]], [R_af[c]])
        ada1_evac(ada_items)
        barrier()
        R_cw = Res()
        R_bw = Res()
        if sample:
            SQ.dma(cxwin[:, :, 0:514], cxsp_s[:, :, AOFF + rp0 - 1:AOFF + rp0 + 513],
                   [], [R_cw])
            bsegs = [(0, 512, 0)]
        else:
            for sg_ in range(2):
                p0 = APOS[sg_] - 1
                SQ.dma(cxwin[:, :, sg_ * 272:sg_ * 272 + 258], cxsp_s[:, :, p0:p0 + 258], [], [R_cw])
            bsegs = [(0, 256, 0), (256, 256, 272)]
        SQ.dma(bgwin[:, :, :], bgsp_s[:, :, tile * 512:(tile + 1) * 512], [], [R_bw])
        ada_items = ada1_slabs(9) if l == 0 else []
        R_ca = Res()
        R_bi = [Res() for _ in range(8)]
        for c in range(8):
            for (o0, ln, w0) in bsegs:
                dve(lambda v, c=c, o0=o0, ln=ln, w0=w0: v.tensor_scalar(
                    out=cacc[:, o0:o0 + ln], in0=cxwin[:, c, w0:w0 + ln], scalar1=cbw[:, l * 8 + c, 0:1],
                    scalar2=None, op0=ALU.mult), [R_cw], [R_ca])
                for j in (1, 2):
                    dve(lambda v, c=c, o0=o0, ln=ln, w0=w0, j=j: v.scalar_tensor_tensor(
                        out=cacc[:, o0:o0 + ln], in0=cxwin[:, c, w0 + j:w0 + j + ln],
                        scalar=cbw[:, l * 8 + c, j:j + 1], in1=cacc[:, o0:o0 + ln], op0=ALU.mult, op1=ALU.add),
                        [R_cw], [R_ca])
            dve(lambda v, c=c: v.tensor_tensor(out=binp[:, c, :], in0=cacc[:, :], in1=bgwin[:, c, :],
                                               op=ALU.mult), [R_ca, R_bw], [R_bi[c]])
        ada1_evac(ada_items)
        barrier()
        R_kw = Res()
        R_vw = Res()
        R_kc = Res()
        R_vc = Res()
        if sample:
            p0 = KOFF + rp0 - 128
            SQ.dma(kwin[:, :, :], ksp_s[:, :, p0:p0 + 768], [], [R_kw])
            SQ.dma(vwin[:, :, :], vsp_s[p0:p0 + 768, :].rearrange("(b p) d -> p b d", p=128),
                   [], [R_vw])
        else:
            SQ.dma(kwin[:, :, 0:512], ksp_s[:, :, 0:512], [], [R_kw])
            SQ.dma(vwin[:, 0:4, :], vsp_s[0:512, :].rearrange("(b p) d -> p b d", p=128), [], [R_vw])
        if sample:
            ctx_load(l, R_kc, R_vc)
        R_q = [Res() for _ in range(4)]
        R_pT = [Res() for _ in range(3)]
        R_dt = Res()
        R_rb = [(Res(), Res(), Res()) for _ in range(2)]
        pcount = 0
        if sample:
            blk0 = (tile - 1) * 4
            lo, hi = (1, 10) if l == 0 else (2, 9)
            qbs = [qb for qb in range(4) if lo <= blk0 + qb <= hi]
        else:
            qbs = [0, 1, 2, 3]
        for g in range(4):
            slabs = []
            for hh in range(2):
                slabs.append(wload1(wv[:, :, g * 512 + hh * 256:g * 512 + (hh + 1) * 256], [KC, 256]))
            for j in range(4):
                slab, R = slabs[j // 2]
                b = nbank()
                mm([(PB[b][:, :], slab[:, kc, (j % 2) * 128:(j % 2 + 1) * 128], hb[0][:, kc, :], kc == 0,
                     kc == KC - 1) for kc in range(KC)], [R] + hall, [PBR[b]])
                if sample:
                    ri = j % 2
                    rope(b, 0, qg[:, j, :], [R_q[j]], rxb2[ri], rt1b[ri], rt2b[ri], R_rb[ri])
                else:
                    act(qg[:, j, :], PB[b][:, :], AF.Copy, [PBR[b]], [R_q[j]])
            for qb in qbs:
                chunks = []
                if sample:
                    for w_ in range(3):
                        kb = blk0 + qb - 1 + w_
                        chunks.append((kwin[:, g, (qb + w_) * 128:(qb + w_ + 1) * 128],
                                       vwin[:, qb + w_, g * 128:(g + 1) * 128],
                                       keybias[:, kb:kb + 1] if 0 <= kb < 12 else None,
                                       (0 if w_ == 0 else (1 if w_ == 2 else None)), [R_kw, R_vw]))
                    for c_ in range(4):
                        chunks.append((kctx[:, g, c_ * 128:(c_ + 1) * 128], vctx[:, c_, g * 128:(g + 1) * 128],
                                       None, None, [R_kc, R_vc]))
                else:
                    sq_ = qb // 2
                    for c_ in range(2):
                        kk = sq_ * 2 + c_
                        chunks.append((kwin[:, g, kk * 128:(kk + 1) * 128], vwin[:, kk, g * 128:(g + 1) * 128],
                                       None, None, [R_kw, R_vw]))
                nch = len(chunks)
                rq = qg[:, :, qb * 128:(qb + 1) * 128]
                for ci, (kap, vap, bias, msk, rds) in enumerate(chunks):
                    b = nbank()
                    mm([(PB[b][:, :].rearrange("p (h q) -> p h q", h=4), kap, rq, True, True)], rds + R_q, [PBR[b]])
                    pi = pcount % 3
                    pcount += 1
                    act(pT[pi][:, :], PB[b][:, :], AF.Exp, [PBR[b]], [R_pT[pi]], bias=bias, scale=SCALE)
                    if msk is not None:
                        dve(lambda v, pi=pi, msk=msk: v.tensor_tensor(out=pT[pi][:, :], in0=pT[pi][:, :],
                                                                      in1=tri[:, msk, :], op=ALU.mult),
                            [R_pT[pi]], [R_pT[pi]])
                    mm([(PB[5][:, :], vap, pT[pi][:, :], ci == 0, ci == nch - 1)], rds + [R_pT[pi]], [PBR[5]])
                    mm([(PB[6][:, :], ones[:, :], pT[pi][:, :], ci == 0, ci == nch - 1)], [R_pT[pi]], [PBR[6]])
                for j in range(4):
                    h_ = l * 16 + 4 * g + j
                    dve(lambda v, j=j, h_=h_: v.tensor_scalar(out=dtmp[:, j * 128:(j + 1) * 128],
                                                              in0=PB[6][:, j * 128:(j + 1) * 128],
                                                              scalar1=sexp[:, h_:h_ + 1], scalar2=None, op0=ALU.add),
                        [PBR[6]], [R_dt])
                dve(lambda v: v.reciprocal(out=dtmp[:, :], in_=dtmp[:, :]), [R_dt], [R_dt])
                dve(lambda v, g=g, qb=qb: v.tensor_tensor(
                    out=attn[:, 4 * g:4 * g + 4, qb * 128:(qb + 1) * 128],
                    in0=PB[5][:, :].rearrange("p (h q) -> p h q", h=4),
                    in1=dtmp[:, :].rearrange("p (h q) -> p h q", h=4), op=ALU.mult), [PBR[5], R_dt], [R_attn])
        barrier()
        R_sg = [Res() for _ in range(3)]
        R_mt = [Res() for _ in range(3)]
        R_mx = [Res() for _ in range(KC)]
        wao = kview(w_ao_d[l])
        wa = kview(w_aout_d[l])
        wb = kview(w_bout_d[l])
        for op_ in range(8):
            c0 = op_ * 256
            slabO, RO = wload1(wao[:, :, c0:c0 + 256], [KC, 256])
            slabAB, RAB = wload([(lambda v: v[:, 0:8, :], wa[:, :, c0:c0 + 256]),
                                 (lambda v: v[:, 8:16, :], wb[:, :, c0:c0 + 256])], [KC, 256])
            slabG = [wload1(wv[:, :, 8192 + gi * 2048 + c0:8192 + gi * 2048 + c0 + 256], [KC, 256])
                     for gi in range(3)]
            for j in range(2):
                oc = op_ * 2 + j
                cs_ = slice(j * 128, (j + 1) * 128)
                for gi in range(3):
                    slab, R = slabG[gi]
                    b = nbank()
                    mm([(PB[b][:, :], slab[:, kc, cs_], hb[0][:, kc, :], kc == 0, kc == KC - 1) for kc in range(KC)],
                       [R] + hall, [PBR[b]])
                    act(sgm[gi][:, :], PB[b][:, :], AF.Sigmoid, [PBR[b]], [R_sg[gi]])
                    b2 = nbank()
                    if gi == 0:
                        mm([(PB[b2][:, :], slabAB[:, c, cs_], afeat[:, c, :], c == 0, c == 7) for c in range(8)],
                           [RAB] + R_af, [PBR[b2]])
                    elif gi == 1:
                        mm([(PB[b2][:, :], slabAB[:, 8 + c, cs_], binp[:, c, :], c == 0, c == 7) for c in range(8)],
                           [RAB] + R_bi, [PBR[b2]])
                    else:
                        mm([(PB[b2][:, :], slabO[:, kc, cs_], attn[:, kc, :], kc == 0, kc == KC - 1)
                            for kc in range(KC)], [RO, R_attn], [PBR[b2]])
                    dve(lambda v, b2=b2, gi=gi: v.tensor_tensor(out=mt[gi][:, :], in0=PB[b2][:, :],
                                                                in1=sgm[gi][:, :], op=ALU.mult),
                        [PBR[b2], R_sg[gi]], [R_mt[gi]])
                dve(lambda v: v.tensor_tensor(out=mt[0][:, :], in0=mt[0][:, :], in1=mt[1][:, :], op=ALU.add),
                    [R_mt[0], R_mt[1]], [R_mt[0]])
                dve(lambda v, oc=oc: v.tensor_tensor(out=mixed[:, oc, :], in0=mt[0][:, :], in1=mt[2][:, :],
                                                     op=ALU.add), [R_mt[0], R_mt[2]], [R_mx[oc]])
        wo = kview(w_out_d[l])
        for op_ in range(8):
            slab, R = wload1(wo[:, :, op_ * 256:(op_ + 1) * 256], [KC, 256])
            for j in range(2):
                oc = op_ * 2 + j
                b = nbank()
                mm([(PB[b][:, :], slab[:, kc, j * 128:(j + 1) * 128], mixed[:, kc, :], kc == 0, kc == KC - 1)
                    for kc in range(KC)], [R] + R_mx, [PBR[b]])
                dve(lambda v, b=b, oc=oc: v.scalar_tensor_tensor(
                    out=xb[0][:, oc, :], in0=PB[b][:, :], scalar=Gsc(l, 1, oc, v_), in1=xb[0][:, oc, :],
                    op0=ALU.mult, op1=ALU.add), [PBR[b]], [R_x[0][oc]])
        store_x(0, tile)
        barrier()

    ctx_sem = es.enter_context(nc.semaphore("ctx"))
    ctx_val = [0]

    def ctx_load(l, R_kc, R_vc):
        GP.deps([], [R_kc, R_vc])
        GP.wait_ev(PE.last())
        GP.wait_ev(ACT.last())
        GP.wait_ev(DVE.last())
        nc.gpsimd.dma_start(out=kctx[:, :, :], in_=kctxT_d[l]).then_inc(ctx_sem, 16)
        nc.gpsimd.dma_start(out=vctx[:, :, :], in_=vctx_d[l].rearrange("(b p) d -> p b d", p=128)).then_inc(ctx_sem, 16)
        ctx_val[0] += 32
        ev = Ev(ctx_sem, ctx_val[0])
        R_kc.set_writer(ev)
        R_vc.set_writer(ev)

    def final_norm(t, tile):
        sumsq([(xb[t][:, kc, :], R_x[t][kc]) for kc in range(KC)])
        rstd_from_bank7(D)
        for kc in range(KC):
            dve(lambda v, kc=kc: v.scalar_tensor_tensor(
                out=xb[t][:, kc, :], in0=xb[t][:, kc, :], scalar=gfin[:, kc:kc + 1], in1=rstd[:, :],
                op0=ALU.mult, op1=ALU.mult), [R_x[t][kc], R_rstd], [R_x[t][kc]])
        rx = [R_x[t][kc] for kc in range(KC)]
        if tile == 0:
            SQ.dma(yT_o[:, :, 0:512], xb[t][:, :, :], rx, [])
        elif tile == 1:
            SQ.dma(yT_o[:, :, 512:768], xb[t][:, :, 256:512], rx, [])
        elif tile == 2:
            SQ.dma(yT_o[:, :, 768:1280], xb[t][:, :, :], rx, [])
        else:
            SQ.dma(yT_o[:, :, 1280:1536], xb[t][:, :, 0:256], rx, [])

    def stop(tag):
        if STOP[0] == tag:
            raise _Stop()

    def _program():
        STS = [(0, 1), (2, 3)]
        stop("ada")
        for st in range(2):
            tiles = STS[st]
            for t in range(2):
                load_x(xT_d, t, tiles[t] * 512)
            stop("load")
            ffn(0, 0, tiles)
            stop("ffn")
            m1(0, tiles)
            stop("m1")
            for t in range(2):
                store_x(t, tiles[t])
            barrier()
        stop("ph1")
        for l in range(NL):
            for tile in range(4):
                ph2(l, tile)
                stop("ph2_%d_%d" % (l, tile))
            for st in range(2):
                tiles = STS[st]
                for t in range(2):
                    load_x(xs_s, t, tiles[t] * 512)
                ffn(l, 1, tiles)
                stop("ffn2_%d_%d" % (l, st))
                if l + 1 < NL:
                    ffn(l + 1, 0, tiles)
                    m1(l + 1, tiles)
                    for t in range(2):
                        store_x(t, tiles[t])
                else:
                    for t in range(2):
                        final_norm(t, tiles[t])
                barrier()

    try:
        _program()
    except _Stop:
        barrier()
        dbg_o = dout("dbg", [128, 2, KC, 512])
        dbh_o = dout("dbh", [128, 2, KC, 512])
        R_dbg = Res()
        for t in range(2):
            SQ.dma(dbg_o[:, t, :, :], xb[t][:, :, :], [R_dbg], [])
        for t in range(2):
            act(xb[t][:, :, :], hb[t][:, :, :], AF.Copy, [], [R_dbg])
        for E in (SP,):
            E.wait_ev(ACT.last())
        for t in range(2):
            SQ.dma(dbh_o[:, t, :, :], xb[t][:, :, :], [], [])
    for e in SQ.all_last():
        SP.wait_ev(e)
    es.close()
    return nc


def _fm(x2d):
    T, Fd = x2d.shape
    return np.ascontiguousarray(x2d.T.reshape(Fd // 128, 128, T).transpose(1, 0, 2))


def _vec_fm(v):
    sh = v.shape
    Fd = sh[-1]
    r = v.reshape(sh[:-1] + (Fd // 128, 128))
    return np.ascontiguousarray(np.moveaxis(r, -1, 0))


_CACHE = {}
_RETURN_MAPS = [False]


def kernel(x_prompt, x_sample, cache_k, cache_v, c, c_ctx, w_ada, b_ada, g_ff1, w_ff1_in, w_ff1_out, g_mix,
           w_in, attn_sink, w_attn_o, conv_a_w, conv_a_b, g_conv_a, w_a_out, conv_b_w, w_b_out, w_out, g_ff2,
           w_ff2_in, w_ff2_out, g_final):
    f32 = np.float32
    A = lambda a: np.ascontiguousarray(np.asarray(a, dtype=f32))
    x_prompt, x_sample, cache_k, cache_v, c, c_ctx = map(A, (x_prompt, x_sample, cache_k, cache_v, c, c_ctx))
    shared = {
        "w_ada": A(w_ada), "w_ff1_in": A(w_ff1_in), "w_ff1_out": A(w_ff1_out), "w_in": A(w_in),
        "w_attn_o": A(w_attn_o), "w_a_out": A(w_a_out), "w_b_out": A(w_b_out), "w_out": A(w_out),
        "w_ff2_in": A(w_ff2_in), "w_ff2_out": A(w_ff2_out),
    }
    shared["badaT"] = _vec_fm(A(b_ada))
    g3 = np.stack([A(g_ff1), A(g_mix), A(g_ff2)], axis=1)
    shared["gT"] = _vec_fm(g3)
    shared["gfin"] = _vec_fm(A(g_final))
    caw = A(conv_a_w)[:, :, 0, :]
    shared["caw"] = np.ascontiguousarray(_vec_fm(caw).transpose(0, 1, 3, 2))
    shared["cab"] = _vec_fm(A(conv_a_b))
    shared["gca"] = _vec_fm(A(g_conv_a))
    cbw = A(conv_b_w)[:, :, 0, :]
    shared["cbw"] = np.ascontiguousarray(_vec_fm(cbw).transpose(0, 1, 3, 2))
    shared["sinkb"] = np.ascontiguousarray(np.broadcast_to(A(attn_sink)[None], (128, NL, 16)))
    shared["onesm"] = np.ones((128, 128), f32)
    Rt = np.zeros((128, 128), f32)
    for base in (0, 64):
        for p in range(32):
            Rt[base + p + 32, base + p] = -1.0
            Rt[base + p, base + p + 32] = 1.0
    shared["Rt"] = Rt
    kl = np.arange(128)[:, None]
    ql = np.arange(128)[None, :]
    tri = np.stack([np.tile((kl >= ql).astype(f32), (1, 4)), np.tile((kl <= ql).astype(f32), (1, 4))], axis=1)
    shared["tri"] = np.ascontiguousarray(tri)
    inv = (np.float32(10000.0) ** (-np.arange(32, dtype=f32) / np.float32(32))).astype(f32)

    in_maps = []
    for i in range(NCORES):
        b = i // 4
        s = (i % 4) * 1024
        gp = s - 256 + np.arange(REG)
        ok = (gp >= 0) & (gp < 4096)
        xs_ = np.zeros((REG, D), f32)
        xs_[ok] = x_sample[b, gp[ok]]
        X = np.concatenate([x_prompt[2 * i], x_prompt[2 * i + 1], xs_], axis=0)
        m = dict(shared)
        m["xT"] = _fm(X)
        m["cT"] = np.ascontiguousarray(np.stack([_vec_fm(c_ctx), _vec_fm(c[b])], axis=-1))
        m["kctxT"] = np.ascontiguousarray(cache_k[b].transpose(0, 3, 2, 1))
        m["vctx"] = np.ascontiguousarray(cache_v[b].reshape(NL, 512, 512))
        gpc = np.clip(gp, 0, 4095)
        row = (gpc // 64).astype(f32)
        col = (gpc % 64).astype(f32)
        ang_r = (row[:, None] * inv[None, :]).astype(f32)
        ang_c = (col[:, None] * inv[None, :]).astype(f32)
        cr, sr = np.cos(ang_r).astype(f32), np.sin(ang_r).astype(f32)
        cc, sc_ = np.cos(ang_c).astype(f32), np.sin(ang_c).astype(f32)
        m["cosT"] = np.ascontiguousarray(np.concatenate([cr, cr, cc, cc], axis=1).T)
        m["sinT"] = np.ascontiguousarray(np.concatenate([sr, sr, sc_, sc_], axis=1).T)
        m["valid"] = np.ascontiguousarray(np.broadcast_to(ok.astype(f32)[None], (128, REG)))
        kb = np.where(ok, 0.0, NEG).astype(f32).reshape(12, 128).T
        m["keybias"] = np.ascontiguousarray(kb)
        in_maps.append(m)

    if _RETURN_MAPS[0]:
        return in_maps
    if "nc" not in _CACHE:
        _CACHE["nc"] = build_program()
    nc = _CACHE["nc"]
    res = run_bass_kernel_spmd(nc, in_maps, core_ids=list(range(NCORES)))
    y_prompt = np.zeros((16, 256, D), f32)
    y_sample = np.zeros((2, 4096, D), f32)
    nk = np.zeros((16, NL, 256, 4, 128), f32)
    nv = np.zeros((16, NL, 256, 4, 128), f32)
    for i in range(NCORES):
        r = res.results[i]
        yT = np.asarray(r["yT"])
        Y = yT.transpose(2, 1, 0).reshape(1536, D)
        y_prompt[2 * i] = Y[0:256]
        y_prompt[2 * i + 1] = Y[256:512]
        b = i // 4
        s = (i % 4) * 1024
        y_sample[b, s:s + 1024] = Y[512:1536]
        kTo = np.asarray(r["kTo"])
        vo = np.asarray(r["vo"])
        for sq_ in range(2):
            nk[2 * i + sq_] = kTo[:, :, :, sq_ * 256:(sq_ + 1) * 256].transpose(0, 3, 2, 1)
            nv[2 * i + sq_] = vo[:, sq_ * 256:(sq_ + 1) * 256, :].reshape(NL, 256, 4, 128)
    return (y_prompt, y_sample, nk, nv)
```
